# Optimizing a Trainium2 kernel written in Bass

```python
import math
import jax, jax.numpy as jnp
from jax import lax
import numpy as np

D_MODEL = 1024
BATCH = 4
SEQ = 8192
DEPTH = 1

D_SSM = 1024
D_ATTN = 1024
D_MIX = D_SSM + D_ATTN

SSM_HEAD_DIM = 64
SSM_HEADS = D_SSM // SSM_HEAD_DIM
SSM_GROUPS = 2
SSM_HPG = SSM_HEADS // SSM_GROUPS
D_STATE = 128
D_CONV = 5
CHUNK = 128
CONV_CH = D_SSM + 2 * SSM_GROUPS * D_STATE

ATTN_HEAD_DIM = 64
ATTN_HEADS = D_ATTN // ATTN_HEAD_DIM
KV_HEADS = 4
Q_PER_KV = ATTN_HEADS // KV_HEADS
WINDOW = 128
BLK = 128
NUM_BUCKETS = 32
MAX_DISTANCE = 128
MAX_EXACT = 8

EPS = 1e-6
NEG_INF = -1e30

PROJ_SIZES = [
    D_SSM,
    CONV_CH,
    2 * SSM_HEADS,
    ATTN_HEADS * ATTN_HEAD_DIM,
    KV_HEADS * ATTN_HEAD_DIM,
    KV_HEADS * ATTN_HEAD_DIM,
    D_ATTN,
]
D_PROJ = int(sum(PROJ_SIZES))
PROJ_SPLITS = [int(s) for s in np.cumsum(PROJ_SIZES)[:-1]]

kernel_name = "hymba_ssd_swa_bidir_layer"


def rms_norm(x, g):
    xf = x.astype(jnp.float32)
    y = xf * lax.rsqrt(jnp.mean(xf * xf, axis=-1, keepdims=True) + EPS)
    return (y * g.astype(jnp.float32)).astype(x.dtype)


def ssd_chunked(xh, dt, A, Bm, Cm):
    b, S = xh.shape[0], xh.shape[1]
    c = S // CHUNK
    X = (xh.astype(jnp.float32) * dt[..., None]).reshape(b, c, CHUNK, SSM_GROUPS, SSM_HPG, SSM_HEAD_DIM)
    a = (dt * A).reshape(b, c, CHUNK, SSM_GROUPS, SSM_HPG).transpose(0, 3, 4, 1, 2)
    cs = jnp.cumsum(a, axis=-1)
    tril = jnp.tril(jnp.ones((CHUNK, CHUNK), dtype=bool))
    L = jnp.exp(jnp.where(tril, cs[..., :, None] - cs[..., None, :], -jnp.inf))
    Bc = Bm.astype(jnp.float32).reshape(b, c, CHUNK, SSM_GROUPS, D_STATE)
    Cc = Cm.astype(jnp.float32).reshape(b, c, CHUNK, SSM_GROUPS, D_STATE)
    CB = jnp.einsum('bclgn,bcsgn->bgcls', Cc, Bc)
    scores = CB[:, :, None] * L
    y_diag = jnp.einsum('bghcls,bcsghp->bclghp', scores, X)
    decay_states = jnp.exp(cs[..., -1:] - cs)
    states = jnp.einsum('bcsgn,bghcs,bcsghp->cbghpn', Bc, decay_states, X)
    chunk_decay = jnp.exp(cs[..., -1]).transpose(3, 0, 1, 2)

    def step(carry, inp):
        st, dec = inp
        return dec[..., None, None] * carry + st, carry

    init = jnp.zeros(states.shape[1:], jnp.float32)
    _, states_in = lax.scan(step, init, (states, chunk_decay))
    y_off = jnp.einsum('bclgn,cbghpn,bghcl->bclghp', Cc, states_in, jnp.exp(cs))
    return (y_diag + y_off).reshape(b, S, SSM_GROUPS, SSM_HPG, SSM_HEAD_DIM)


def t5_buckets(rel):
    half = NUM_BUCKETS // 2
    ret = (rel > 0).astype(jnp.int32) * half
    n = jnp.abs(rel)
    is_small = n < MAX_EXACT
    nf = jnp.maximum(n, 1).astype(jnp.float32)
    large = MAX_EXACT + (jnp.log(nf / MAX_EXACT) / math.log(MAX_DISTANCE / MAX_EXACT)
                         * (half - MAX_EXACT)).astype(jnp.int32)
    large = jnp.minimum(large, half - 1)
    return ret + jnp.where(is_small, n, large)


def windowed_gqa(q, k, v, rel_bias, sink):
    b, S = q.shape[0], q.shape[1]
    nb = S // BLK
    scale = ATTN_HEAD_DIM ** -0.5
    qi = jnp.arange(BLK)[:, None]
    t = jnp.arange(3 * BLK)[None, :]
    rel = t - BLK - qi
    in_window = jnp.abs(rel) <= WINDOW
    bias = rel_bias.astype(jnp.float32)[t5_buckets(rel)]
    bias = bias.transpose(2, 0, 1).reshape(KV_HEADS, Q_PER_KV, BLK, 3 * BLK)
    s = sink.astype(jnp.float32).reshape(KV_HEADS, Q_PER_KV)[None, :, :, None, None]

    pad = ((0, 0), (BLK, BLK), (0, 0), (0, 0))
    k_pad = jnp.pad(k, pad).astype(jnp.float32)
    v_pad = jnp.pad(v, pad).astype(jnp.float32)
    q_blocks = q.astype(jnp.float32).reshape(b, nb, BLK, KV_HEADS, Q_PER_KV, ATTN_HEAD_DIM)
    q_blocks = q_blocks.transpose(1, 0, 2, 3, 4, 5)

    def block(args):
        qb, idx = args
        kb = lax.dynamic_slice_in_dim(k_pad, idx * BLK, 3 * BLK, axis=1)
        vb = lax.dynamic_slice_in_dim(v_pad, idx * BLK, 3 * BLK, axis=1)
        key_pos = idx * BLK - BLK + t
        valid = in_window & (key_pos >= 0) & (key_pos < S)
        logits = jnp.einsum('bqkgd,btkd->bkgqt', qb, kb) * scale + bias
        logits = jnp.where(valid, logits, NEG_INF)
        m = jnp.maximum(jnp.max(logits, axis=-1, keepdims=True), s)
        p = jnp.exp(logits - m)
        denom = jnp.sum(p, axis=-1, keepdims=True) + jnp.exp(s - m)
        o = jnp.einsum('bkgqt,btkd->bqkgd', p, vb)
        return o / denom.transpose(0, 3, 1, 2, 4)

    out = lax.map(block, (q_blocks, jnp.arange(nb)))
    out = out.transpose(1, 0, 2, 3, 4, 5).reshape(b, S, ATTN_HEADS * ATTN_HEAD_DIM)
    return out.astype(q.dtype)


def hybrid_layer(x, norm_in_g, w_in, conv_w, conv_b, dt_bias, a_log, d_skip,
                 ssd_norm_g, rel_bias, sink, attn_norm_g, w_out):
    b, S, _ = x.shape
    h = rms_norm(x, norm_in_g)
    proj = jnp.einsum('bsd,de->bse', h, w_in)
    z, xbc, dt_raw, q, k, v, ga = jnp.split(proj, PROJ_SPLITS, axis=-1)

    xbc = lax.conv_general_dilated(
        xbc, conv_w.reshape(D_CONV, 1, CONV_CH), window_strides=(1,),
        padding=[((D_CONV - 1) // 2, (D_CONV - 1) // 2)],
        dimension_numbers=('NWC', 'WIO', 'NWC'), feature_group_count=CONV_CH)
    xbc = jax.nn.silu(xbc + conv_b)
    xs, Bm, Cm = jnp.split(xbc, [D_SSM, D_SSM + SSM_GROUPS * D_STATE], axis=-1)
    xh = xs.reshape(b, S, SSM_GROUPS, SSM_HPG, SSM_HEAD_DIM)
    Bm = Bm.reshape(b, S, SSM_GROUPS, D_STATE)
    Cm = Cm.reshape(b, S, SSM_GROUPS, D_STATE)
    dt = jax.nn.softplus(dt_raw.astype(jnp.float32).reshape(b, S, 2, SSM_HEADS)
                         + dt_bias.astype(jnp.float32))
    A = -jnp.exp(a_log.astype(jnp.float32))
    dt_f = dt[:, :, 0].reshape(b, S, SSM_GROUPS, SSM_HPG)
    dt_b = dt[:, :, 1].reshape(b, S, SSM_GROUPS, SSM_HPG)
    A_f = A[0].reshape(SSM_GROUPS, SSM_HPG)
    A_b = A[1].reshape(SSM_GROUPS, SSM_HPG)
    rev = lambda a: a[:, ::-1]
    y_fwd = ssd_chunked(xh, dt_f, A_f, Bm, Cm)
    y_bwd = rev(ssd_chunked(rev(xh), rev(dt_b), A_b, rev(Bm), rev(Cm)))
    y = y_fwd + y_bwd + d_skip.astype(jnp.float32).reshape(SSM_GROUPS, SSM_HPG)[..., None] * xh.astype(jnp.float32)
    y = y.reshape(b, S, SSM_GROUPS, D_SSM // SSM_GROUPS)
    zg = jax.nn.silu(z.astype(jnp.float32)).reshape(b, S, SSM_GROUPS, D_SSM // SSM_GROUPS)
    y_ssd = rms_norm(y * zg, ssd_norm_g.reshape(SSM_GROUPS, D_SSM // SSM_GROUPS))
    y_ssd = y_ssd.reshape(b, S, D_SSM).astype(x.dtype)

    qh = q.reshape(b, S, ATTN_HEADS, ATTN_HEAD_DIM)
    kh = k.reshape(b, S, KV_HEADS, ATTN_HEAD_DIM)
    vh = v.reshape(b, S, KV_HEADS, ATTN_HEAD_DIM)
    o = windowed_gqa(qh, kh, vh, rel_bias, sink)
    y_attn = rms_norm(o * jax.nn.silu(ga), attn_norm_g)

    cat = jnp.concatenate([y_ssd, y_attn.astype(x.dtype)], axis=-1)
    return x + jnp.einsum('bse,ed->bsd', cat, w_out)


def setup_inputs(seed: int = 0) -> dict:
    key = jax.random.key(seed)
    ks = jax.random.split(key, 16)
    f32 = jnp.float32
    x = jax.random.normal(ks[0], (BATCH, SEQ, D_MODEL), f32)
    norm_in_g = 1.0 + 0.02 * jax.random.normal(ks[1], (D_MODEL,), f32)
    w_in = jax.random.normal(ks[2], (D_MODEL, D_PROJ), f32) * D_MODEL ** -0.5
    conv_w = jax.random.normal(ks[3], (D_CONV, CONV_CH), f32) * D_CONV ** -0.5
    conv_b = 0.02 * jax.random.normal(ks[4], (CONV_CH,), f32)
    dt0 = jnp.exp(jax.random.uniform(ks[5], (2, SSM_HEADS), f32,
                                     math.log(1e-3), math.log(1e-1)))
    dt_bias = dt0 + jnp.log(-jnp.expm1(-dt0))
    a_log = jnp.log(jax.random.uniform(ks[6], (2, SSM_HEADS), f32, 1.0, 16.0))
    d_skip = 1.0 + 0.1 * jax.random.normal(ks[7], (SSM_HEADS,), f32)
    ssd_norm_g = 1.0 + 0.02 * jax.random.normal(ks[8], (D_SSM,), f32)
    rel_bias = 0.1 * jax.random.normal(ks[9], (NUM_BUCKETS, ATTN_HEADS), f32)
    sink = 0.5 * jax.random.normal(ks[10], (ATTN_HEADS,), f32)
    attn_norm_g = 1.0 + 0.02 * jax.random.normal(ks[11], (D_ATTN,), f32)
    w_out = jax.random.normal(ks[12], (DEPTH, D_MIX, D_MODEL), f32) * D_MIX ** -0.5
    final_norm_g = 1.0 + 0.02 * jax.random.normal(ks[13], (D_MODEL,), f32)
    return {"x": x, "norm_in_g": norm_in_g, "w_in": w_in, "conv_w": conv_w,
            "conv_b": conv_b, "dt_bias": dt_bias, "a_log": a_log, "d_skip": d_skip,
            "ssd_norm_g": ssd_norm_g, "rel_bias": rel_bias, "sink": sink,
            "attn_norm_g": attn_norm_g, "w_out": w_out, "final_norm_g": final_norm_g}


def reference(x, norm_in_g, w_in, conv_w, conv_b, dt_bias, a_log, d_skip,
              ssd_norm_g, rel_bias, sink, attn_norm_g, w_out, final_norm_g):
    h = x
    for layer in range(DEPTH):
        h = hybrid_layer(h, norm_in_g, w_in, conv_w, conv_b, dt_bias, a_log, d_skip,
                         ssd_norm_g, rel_bias, sink, attn_norm_g, w_out[layer])
    return rms_norm(h, final_norm_g)
```

```python
import math
import os
import numpy as np
import ml_dtypes
import concourse.bass as bass
import concourse.mybir as mybir
from concourse.bass_utils import run_bass_kernel_spmd

F32 = mybir.dt.float32
BF16 = mybir.dt.bfloat16
AF = mybir.ActivationFunctionType
ALU = mybir.AluOpType

T = 128
D = 1024
NTH_FULL = 32
TM_Z, TM_GA, TM_V, TM_DT, FM = 0, 1024, 2048, 2304, 2336
NCOLS = 5408
EPS = 1e-6
NEG = -30000.0

C_M01F, C_M01B, C_DTB, C_ALOG, C_SINK, C_FG = 0, 128, 256, 288, 320, 336
C_GIN, C_GSSD, C_GATT, C_CB, C_CW, C_DSK, C_NH, NCF = 1360, 1368, 1376, 1384, 1396, 1456, 1464, 1472
B_ID, B_TRF, B_TRB, B_ONE, B_MNF, B_MNB, NCB = 0, 128, 256, 384, 512, 1024, 1536


class _Buf:
    __slots__ = ("name", "last_w", "readers", "excl")

    def __init__(self, name, excl=False):
        self.name = name
        self.last_w = None
        self.readers = []
        self.excl = excl


class _Sched:
    NDMA = 8

    def __init__(self, nc):
        self.nc = nc
        self.eng = {"pe": nc.tensor, "act": nc.scalar, "dve": nc.vector, "pool": nc.gpsimd, "sp": nc.sync}
        names = list(self.eng) + ["dma%d" % j for j in range(self.NDMA)]
        self.sem = {n: nc.alloc_semaphore("s_" + n) for n in names}
        self.cnt = {n: 0 for n in names}
        self.inc = {n: (16 if n.startswith("dma") else 1) for n in names}
        self.seen = {e: {n: 0 for n in names} for e in self.eng}
        self.rr = 0

    def _deps(self, e, reads, writes):
        deps = {}

        def need(dep, raw):
            if dep is None:
                return
            f, idx = dep
            if f == e and (e == "pe" or not raw):
                return
            if deps.get(f, 0) < idx:
                deps[f] = idx

        for b in reads:
            need(b.last_w, True)
            if b.excl:
                for r in b.readers:
                    need(r, False)
        for b in writes:
            need(b.last_w, False)
            for r in b.readers:
                need(r, False)
        return deps

    warm_fn = None
    _in_warm = False

    def _wait(self, e, deps):
        eng = self.eng[e]
        if e == "pe" and self.warm_fn is not None and not self._in_warm:
            if any(self.seen[e][f] < idx for f, idx in deps.items()):
                self._in_warm = True
                self.warm_fn()
                self._in_warm = False
        for f, idx in deps.items():
            if self.seen[e][f] >= idx:
                continue
            eng.wait_ge(self.sem[f], idx * self.inc[f])
            self.seen[e][f] = idx

    def _mark(self, me, reads, writes):
        for b in writes:
            b.last_w = me
            b.readers = []
        for b in reads:
            if b not in writes:
                b.readers.append(me)

    def op(self, e, fn, reads=(), writes=()):
        self._wait(e, self._deps(e, reads, writes))
        inst = fn(self.eng[e])
        self.cnt[e] += 1
        inst.then_inc(self.sem[e], 1)
        self._mark((e, self.cnt[e]), reads, writes)

    def dma(self, out, in_, reads=(), writes=()):
        d = "dma%d" % self.rr
        self.rr = (self.rr + 1) % self.NDMA
        deps = self._deps("sp", reads, writes)
        if self.cnt[d] > 0:
            deps[d] = max(deps.get(d, 0), self.cnt[d])
        self._wait("sp", deps)
        inst = self.nc.sync.dma_start(out=out, in_=in_)
        self.cnt[d] += 1
        inst.then_inc(self.sem[d], 16)
        self._mark((d, self.cnt[d]), reads, writes)

    def finish(self):
        deps = {n: c for n, c in self.cnt.items() if c > 0 and n != "sp"}
        self._wait("sp", deps)


def _run(gens, bg=None):
    gens = [g for g in gens if g is not None]
    while gens:
        for g in list(gens):
            try:
                next(g)
            except StopIteration:
                gens.remove(g)
        if bg is not None:
            try:
                next(bg)
            except StopIteration:
                bg = None


def _build(NTH):
    NT = 2 * NTH
    nc = bass.Bass("TRN2", target_bir_lowering=False, dynamic_dma_scratch_size=512)
    x_d = nc.dram_tensor("x", [NT * T, D], F32, kind="ExternalInput").ap()
    w_d = nc.dram_tensor("w_in", [D, NCOLS], F32, kind="ExternalInput").ap()
    wo_d = nc.dram_tensor("w_out", [2 * D, D], F32, kind="ExternalInput").ap()
    cf_d = nc.dram_tensor("cst_f32", [128, NCF], F32, kind="ExternalInput").ap()
    cb_d = nc.dram_tensor("cst_bf", [128, NCB], BF16, kind="ExternalInput").ap()
    bt_d = nc.dram_tensor("biasT", [128, 3 * 16 * 128], BF16, kind="ExternalInput").ap()
    out_d = nc.dram_tensor("out", [NTH * T, D], F32, kind="ExternalOutput").ap()
    yb_d = nc.dram_tensor("ybwd", [NTH * T, D], F32, kind="Internal").ap()
    xbc_d = nc.dram_tensor("xbcs", [NTH, 128, 1536], BF16, kind="Internal").ap()

    def sb(name, shape, dt):
        return nc.alloc_sbuf_tensor(name, shape, dt)

    w_sb = sb("w_sb", [128, 8, NCOLS], BF16)
    wo_sb = sb("wo_sb", [128, 16, D], BF16)
    biasT = sb("biasT_sb", [128, 3, 16, 128], BF16)
    dconv = sb("dconv", [128, 60, 128], BF16)
    dskd = sb("dskd", [128, 8, 128], BF16)
    cf = sb("cf", [128, NCF], F32)
    cb = sb("cb", [128, NCB], BF16)
    xf = sb("xf", [128, D], F32)
    Sst = sb("Sst", [128, D], F32)
    y1 = sb("y1", [128, D], F32)
    t2 = sb("t2", [128, D], F32)
    o1 = sb("o1", [128, D], F32)
    hbf = sb("hbf", [128, D], BF16)
    jk = sb("jk", [128, D], BF16)
    hT = [sb("hT%d" % i, [128, 8, T], BF16) for i in range(2)]
    xpad = [sb("xpad%d" % i, [128, 12, 132], BF16) for i in range(2)]
    xbcT = sb("xbcT", [128, 12, T], BF16)
    QT = sb("QT", [128, 8, T], BF16)
    KT = [sb("KT%d" % i, [128, 4, T], BF16) for i in range(3)]
    Va = [sb("Va%d" % i, [128, 4, 65], BF16) for i in range(3)]
    zs = sb("zs", [128, D], BF16)
    gas = sb("gas", [128, D], BF16)
    XX = sb("XX", [128, 2, D], BF16)
    Bt = sb("Bt", [128, 2, T], BF16)
    Lt = [sb("Lt%d" % i, [128, 4, T], BF16) for i in range(2)]
    CBm = sb("CBm", [128, 2, T], F32)
    Sbf = sb("Sbf", [128, D], BF16)
    pT = [sb("pT%d" % i, [128, 1024], BF16) for i in range(2)]
    catbf = sb("catbf", [128, 2 * D], BF16)
    sm = sb("sm", [128, 384], F32)
    smb = sb("smb", [128, 64], BF16)
    A_bc = sb("A_bc", [128, 32], F32)
    esink = sb("esink", [128, 16], F32)

    HN = ["hT0", "hT1", "QT"]
    XN = ["xpad0", "xpad1", "catbf"]
    hT.append(QT)
    xpad.append(catbf[:, 0:12 * 132].rearrange("p (a b) -> p a b", a=12))

    PAB = nc.alloc_psum_tensor("PAB", [128, 1024], F32)
    P = {n: nc.alloc_psum_tensor(n, [128, 512], F32) for n in ["PS", "PY", "OA", "OB", "OC"]}
    P["PA"] = PAB[:, 0:512]
    P["PB"] = PAB[:, 512:1024]
    PT = nc.alloc_psum_tensor("PT", [128, 1024], BF16)

    S = _Sched(nc)
    B = {}

    def buf(name, excl=False):
        if name not in B:
            B[name] = _Buf(name, excl)
        return B[name]

    for n in list(P) + ["PT"]:
        buf(n, True)

    def bk(n):
        return B[n]

    def mm(outap, lhsT, rhs, start, reads, bank):
        S.op("pe", lambda e: e.matmul(outap, lhsT, rhs, start=start, stop=True, skip_group_check=True),
             reads=[buf(r) for r in reads], writes=[bk(bank)])

    def tr(outap, inap, reads, bank):
        S.op("pe", lambda e: e.transpose(outap, inap, cb[:, B_ID:B_ID + 128]),
             reads=[buf(r) for r in reads] + [buf("cb")], writes=[bk(bank)])

    def act(outap, inap, func, reads, writes, **kw):
        S.op("act", lambda e: e.activation(outap, inap, func, **kw),
             reads=[buf(r) for r in reads], writes=[buf(w) for w in writes])

    def tt(e, outap, a, b_, op, reads, writes):
        S.op(e, lambda g: g.tensor_tensor(outap, a, b_, op),
             reads=[buf(r) for r in reads], writes=[buf(w) for w in writes])

    def ts(e, outap, a, s1, s2, op0, op1, reads, writes):
        S.op(e, lambda g: g.tensor_scalar(outap, a, s1, s2, op0, op1),
             reads=[buf(r) for r in reads], writes=[buf(w) for w in writes])

    def cp(e, outap, inap, reads, writes):
        if e == "act":
            S.op(e, lambda g: g.activation(outap, inap, AF.Copy),
                 reads=[buf(r) for r in reads], writes=[buf(w) for w in writes])
        else:
            S.op(e, lambda g: g.tensor_copy(outap, inap),
                 reads=[buf(r) for r in reads], writes=[buf(w) for w in writes])

    def scale_cast(e, outap, inap, col, extra, reads, writes):
        if e == "act":
            if extra != 1.0:
                raise ValueError
            act(outap, inap, AF.Copy, reads + ["cf"], writes, scale=cf[:, col:col + 1])
        elif extra != 1.0:
            ts(e, outap, inap, cf[:, col:col + 1], extra, ALU.mult, ALU.mult, reads + ["cf"], writes)
        else:
            ts(e, outap, inap, cf[:, col:col + 1], None, ALU.mult, ALU.bypass, reads + ["cf"], writes)

    def dma(outap, inap, reads, writes):
        S.dma(outap, inap, reads=[buf(r) for r in reads], writes=[buf(w) for w in writes])

    def bc3(ap2, n):
        return ap2.unsqueeze(2).to_broadcast([128, ap2.shape[1], n])

    NWARM = int(os.environ.get("KWARM", "1"))

    def warm():
        for _ in range(NWARM):
            mm(P["OC"][:, 256:512], cb[:, B_ID:B_ID + 128], cb[:, B_MNF:B_MNF + 256], False, ["cb"], "OC")

    def wname(c0):
        return "w_rev" if 2304 <= c0 < 3872 else "w_fwd"

    dma(cf[:], cf_d, [], ["cf"])
    dma(cb[:], cb_d, [], ["cb"])
    q0, q1 = FM + 1536, FM + 2560
    kq = [0]

    def load_w(kc, c0, w, st, sn):
        dma(st[:, 0:w], w_d[kc * 128:(kc + 1) * 128, c0:c0 + w], [], [sn])
        isq = (q0 <= c0 < q1)
        assert isq == (q0 <= c0 + w - 1 < q1)
        kq[0] += 1
        e = "dve" if (isq or kq[0] % 2) else "act"
        scale_cast(e, w_sb[:, kc, c0:c0 + w], st[:, 0:w], C_GIN + kc, 0.125 if isq else 1.0, [sn], [wname(c0)])

    stg = [(xf, "xf"), (y1, "y1"), (t2, "t2"), (o1, "o1")]
    k = 0
    for kc in range(8):
        for (c0, w) in [(2304, 784), (3088, 784)]:
            st, sn = stg[k % 4]
            k += 1
            load_w(kc, c0, w, st, sn)
    for c in range(12):
        for kk in range(5):
            j = c * 5 + kk
            scale_cast("dve" if j % 2 else "act", dconv[:, j, :], cb[:, B_ID:B_ID + 128], C_CW + j, 1.0,
                       ["cb"], ["dconv"])
    for c in range(8):
        scale_cast("dve", dskd[:, c, :], cb[:, B_ID:B_ID + 128], C_DSK + c, 1.0, ["cb"], ["dskd"])
    act(A_bc[:], cf[:, C_ALOG:C_ALOG + 32], AF.Exp, ["cf"], ["A_bc"])
    ts("dve", A_bc[:], A_bc[:], -1.0, None, ALU.mult, ALU.bypass, ["A_bc"], ["A_bc"])
    act(esink[:], cf[:, C_SINK:C_SINK + 16], AF.Exp, ["cf"], ["esink"])
    for i in range(3):
        S.op("pool", lambda g, i=i: g.memset(Va[i][:, :, 64:65], 1.0), writes=[buf("Va%d" % i)])
    for i in range(2):
        S.op("pool", lambda g, i=i: g.memset(xpad[i][:], 0.0), writes=[buf("xpad%d" % i)])
    S.op("pool", lambda g: g.memset(Sst[:], 0.0), writes=[buf("Sst")])
    S.op("pool", lambda g: g.memset(Sbf[:], 0.0), writes=[buf("Sbf")])
    S.op("pool", lambda g: g.memset(catbf[:], 0.0), writes=[buf("catbf")])
    S.op("dve", lambda g: g.memset(P["OC"][:], 0.0), writes=[bk("OC")])

    def bg_setup():
        st3 = [(y1, "y1"), (t2, "t2"), (o1, "o1")]
        kk_ = 0
        dma(biasT[:].rearrange("p a b c -> p (a b c)"), bt_d, [], ["biasT"])
        yield
        cols = [(0, 1024), (1024, 1024), (2048, 256), (3872, 1024), (4896, 512)]
        for kc in range(8):
            for (c0, w) in cols:
                st, sn = st3[kk_ % 3]
                kk_ += 1
                load_w(kc, c0, w, st, sn)
                yield
        for kc in range(16):
            st, sn = st3[kk_ % 3]
            kk_ += 1
            dma(st[:], wo_d[kc * 128:(kc + 1) * 128, :], [], [sn])
            gcol = (C_GSSD + kc) if kc < 8 else (C_GATT + kc - 8)
            scale_cast("dve" if kk_ % 2 else "act", wo_sb[:, kc, :], st[:], gcol, 1.0, [sn], ["wo_sb"])
            yield

    rot = [0]

    def nextbank():
        rot[0] ^= 1
        return "PA" if rot[0] else "PB"

    def rstd_of(ss_ap, n, out_ap, rd, wr, eps=EPS):
        ts("dve", out_ap, ss_ap, 1.0 / n, eps, ALU.mult, ALU.add, rd, wr)
        k_ = out_ap.shape[1]
        tt("pool", out_ap, out_ap, cf[:, C_NH:C_NH + 1].to_broadcast([128, k_]), ALU.pow, wr + ["cf"], wr)

    def sumsq(src_ap, n_el, out_col, rd, wr):
        act(jk[:, 0:n_el], src_ap, AF.Square, rd + ["jk"], ["jk", wr], accum_out=sm[:, out_col:out_col + 1])

    def front(i, slot, want_kv, head_only=False):
        hs = HN[slot]
        dma(xf[:], x_d[i * T:(i + 1) * T, :], [], ["xf"])
        sumsq(xf[:], D, 0, ["xf"], "sm_ss")
        rstd_of(sm[:, 0:1], D, sm[:, 1:2], ["sm_ss"], ["sm_rs"])
        act(hbf[:], xf[:], AF.Copy, ["xf", "sm_rs"], ["hbf"], scale=sm[:, 1:2])
        yield
        if head_only == "a":
            return
        yield from front_b(i, slot, want_kv, head_only)

    def front_b(i, slot, want_kv, head_only=False):
        hs = HN[slot]
        for kc in range(8):
            tr(PT[:, kc * 128:(kc + 1) * 128], hbf[:, kc * 128:(kc + 1) * 128], ["hbf"], "PT")
        cp("dve", hT[slot][:].rearrange("p a b -> p (a b)"), PT[:], ["PT"], [hs])
        yield
        if head_only:
            return
        yield from front_tail(i, slot, want_kv)

    def front_tail(i, slot, want_kv):
        hs = HN[slot]
        xs_ = XN[slot]
        nch = 10 if i > NTH else 12
        for r in range(3):
            bn = nextbank()
            ncr = min(4, nch - r * 4)
            for c in range(ncr):
                ch = r * 4 + c
                for kc in range(8):
                    mm(P[bn][:, c * 128:(c + 1) * 128], w_sb[:, kc, FM + ch * 128:FM + (ch + 1) * 128],
                       hT[slot][:, kc, :], (c == 0 and kc == 0), ["w_rev", hs], bn)
            cp("act" if r == 1 else "dve", xpad[slot][:, r * 4:r * 4 + ncr, 2:130],
               P[bn][:, 0:ncr * 128].rearrange("p (a b) -> p a b", a=ncr), [bn], [xs_])
            yield
        if want_kv:
            yield from front_kv(i, slot)

    def front_kv(i, slot):
        hs = HN[slot]
        if True:
            ks = i % 3
            bn = nextbank()
            for c in range(4):
                for kc in range(8):
                    mm(P[bn][:, c * 128:(c + 1) * 128], w_sb[:, kc, FM + 2560 + c * 128:FM + 2560 + (c + 1) * 128],
                       hT[slot][:, kc, :], (c == 0 and kc == 0), ["w_fwd", hs], bn)
            cp("act", KT[ks][:].rearrange("p a b -> p (a b)"), P[bn][:], [bn], ["KT%d" % ks])
            yield
            bn = nextbank()
            for kc in range(8):
                mm(P[bn][:, 0:256], hT[slot][:, kc, :], w_sb[:, kc, TM_V:TM_V + 256], kc == 0, ["w_fwd", hs], bn)
            cp("dve", Va[ks][:, :, 0:64], P[bn][:, 0:256].rearrange("p (a b) -> p a b", a=4), [bn], ["Va%d" % ks])
            yield

    def halo(cur, nxt, nxt_precedes):
        a, b_ = XN[cur], XN[nxt]
        if nxt_precedes:
            cp("pool", xpad[cur][:, :, 0:2], xpad[nxt][:, :, 128:130], [b_], [a])
            cp("pool", xpad[nxt][:, :, 130:132], xpad[cur][:, :, 2:4], [a], [b_])
        else:
            cp("pool", xpad[cur][:, :, 130:132], xpad[nxt][:, :, 2:4], [b_], [a])
            cp("pool", xpad[nxt][:, :, 0:2], xpad[cur][:, :, 128:130], [a], [b_])

    def zero_halo(slot, left):
        sl = slice(0, 2) if left else slice(130, 132)
        S.op("pool", lambda g: g.memset(xpad[slot][:, :, sl], 0.0), writes=[buf(XN[slot])])

    def conv_silu(slot, nchunks):
        xs_ = XN[slot]
        chunks = list(range(nchunks))
        for r0 in range(0, len(chunks), 4):
            grp = chunks[r0:r0 + 4]
            bn = nextbank()
            for ci, ch in enumerate(grp):
                for kk in range(5):
                    mm(P[bn][:, ci * 128:(ci + 1) * 128], dconv[:, ch * 5 + kk, :], xpad[slot][:, ch, kk:kk + 128],
                       (ci == 0 and kk == 0), ["dconv", xs_], bn)
            for ci, ch in enumerate(grp):
                act(xbcT[:, ch, :], P[bn][:, ci * 128:(ci + 1) * 128], AF.Silu, [bn, "cf"], ["xbcT"],
                    bias=cf[:, C_CB + ch:C_CB + ch + 1])

    def smc(ss, off, n=16):
        b0 = 32 + ss * 160 + off
        return sm[:, b0:b0 + n]

    def dt_and_decay(slot, dcol, bwd, ss=None):
        ss = slot if ss is None else ss
        hs = HN[slot]
        n_ = lambda t_: "sm_%s%d" % (t_, ss)
        sb_ = "smb%d" % ss
        ahi, alo = smb[:, ss * 32:ss * 32 + 16], smb[:, ss * 32 + 16:ss * 32 + 32]
        for kc in range(8):
            mm(P["PS"][:, 0:16], hT[slot][:, kc, :], w_sb[:, kc, TM_DT + dcol:TM_DT + dcol + 16], kc == 0,
               ["w_rev", hs], "PS")
        tt("dve", smc(ss, 128), P["PS"][:, 0:16], cf[:, C_DTB + dcol:C_DTB + dcol + 16], ALU.add,
           ["PS", "cf"], [n_("t")])
        act(smc(ss, 128), smc(ss, 128), AF.Exp, [n_("t")], [n_("t")])
        act(smc(ss, 0), smc(ss, 128), AF.Ln, [n_("t")], [n_("dt")], bias=1.0)
        yield
        tt("dve", smc(ss, 16), smc(ss, 0), A_bc[:, dcol:dcol + 16], ALU.mult, [n_("dt"), "A_bc"], [n_("a")])
        cp("dve", ahi, smc(ss, 16), [n_("a")], [sb_])
        tt("dve", smc(ss, 128), smc(ss, 16), ahi, ALU.subtract, [n_("a"), sb_], [n_("t")])
        cp("dve", alo, smc(ss, 128), [n_("t")], [sb_])
        yield
        tri = cb[:, B_TRB:B_TRB + 128] if bwd else cb[:, B_TRF:B_TRF + 128]
        mm(P["PS"][:, 0:16], tri, ahi, True, ["cb", sb_], "PS")
        mm(P["PS"][:, 0:16], tri, alo, False, ["cb", sb_], "PS")
        mm(P["PS"][:, 16:32], cb[:, B_ONE:B_ONE + 128], ahi, False, ["cb", sb_], "PS")
        mm(P["PS"][:, 16:32], cb[:, B_ONE:B_ONE + 128], alo, False, ["cb", sb_], "PS")
        cp("dve", smc(ss, 32), P["PS"][:, 0:16], ["PS"], [n_("cs")])
        ts("dve", smc(ss, 48), P["PS"][:, 0:16], -1.0, None, ALU.mult, ALU.bypass, ["PS"], [n_("ncs")])
        tt("dve", smc(ss, 128), P["PS"][:, 16:32], smc(ss, 32), ALU.subtract, ["PS", n_("cs")], [n_("t")])
        act(smc(ss, 64), P["PS"][:, 0:16], AF.Exp, ["PS"], [n_("ecs")])
        act(smc(ss, 96), P["PS"][:, 16:32], AF.Exp, ["PS"], [n_("etot")])
        act(smc(ss, 80), smc(ss, 128), AF.Exp, [n_("t")], [n_("dec")])
        yield
        tt("dve", smc(ss, 112), smc(ss, 0), smc(ss, 80), ALU.mult, [n_("dt"), n_("dec")], [n_("dtdec")])
        yield

    def ssd(ss, bwd, states_only, with_skip):
        n_ = lambda t_: "sm_%s%d" % (t_, ss)
        sb_ = "smb%d" % ss
        for c in range(8):
            tr(PT[:, c * 128:(c + 1) * 128], xbcT[:, c, :], ["xbcT"], "PT")
        ptv = PT[:].rearrange("p (h d) -> p h d", h=16)
        if not states_only:
            tt("dve", XX[:, 0, :].rearrange("p (h d) -> p h d", h=16), ptv, bc3(smc(ss, 0), 64), ALU.mult,
               ["PT", n_("dt")], ["XX"])
        tt("dve", XX[:, 1, :].rearrange("p (h d) -> p h d", h=16), ptv, bc3(smc(ss, 112), 64), ALU.mult,
           ["PT", n_("dtdec")], ["XX"])
        for g in range(2):
            tr(PT[:, g * 128:(g + 1) * 128], xbcT[:, 8 + g, :], ["xbcT"], "PT")
        cp("dve", Bt[:].rearrange("p a b -> p (a b)"), PT[:, 0:256], ["PT"], ["Bt"])
        yield
        if not states_only:
            for g in range(2):
                mm(P["PS"][:, 128 + g * 128:128 + (g + 1) * 128], xbcT[:, 8 + g, :], xbcT[:, 10 + g, :],
                   g == 0, ["xbcT"], "PS")
            mcol = C_M01B if bwd else C_M01F
            tt("dve", CBm[:], P["PS"][:, 128:384].rearrange("p (a b) -> p a b", a=2),
               cf[:, mcol:mcol + 128].unsqueeze(1).to_broadcast([128, 2, 128]), ALU.mult, ["PS", "cf"], ["CBm"])
            yield
            mn = B_MNB if bwd else B_MNF
            tri = cb[:, B_TRB:B_TRB + 128] if bwd else cb[:, B_TRF:B_TRF + 128]
            def grp(g, pyb, lsfix):
                if with_skip:
                    for c in range(4):
                        mm(P[pyb][:, c * 128:(c + 1) * 128], xbcT[:, g * 4 + c, :], dskd[:, g * 4 + c, :],
                           c == 0, ["xbcT", "dskd"], pyb)
                def hb_a(hb):
                    bn = nextbank()
                    mm(P[bn][:], cb[:, B_ID:B_ID + 128], cb[:, mn:mn + 512], True, ["cb"], bn)
                    for j in range(4):
                        h = hb * 4 + j
                        for part in range(2):
                            col = ss * 32 + part * 16 + h
                            mm(P[bn][:, j * 128:(j + 1) * 128], smb[:, col:col + 1].to_broadcast([128, 128]), tri,
                               False, ["cb", sb_], bn)
                    ls = (hb % 2) if lsfix is None else lsfix
                    ln_ = "Lt%d" % ls
                    for j in range(4):
                        h = hb * 4 + j
                        act(Lt[ls][:, j, :], P[bn][:, j * 128:(j + 1) * 128], AF.Exp, [bn, n_("ncs")], [ln_],
                            bias=smc(ss, 48 + h, 1))
                    tt("dve", Lt[ls][:], Lt[ls][:], CBm[:, g:g + 1, :].to_broadcast([128, 4, 128]), ALU.mult,
                       [ln_, "CBm"], [ln_])

                def hb_b(hb):
                    ls = (hb % 2) if lsfix is None else lsfix
                    ln_ = "Lt%d" % ls
                    for j in range(4):
                        h = hb * 4 + j
                        first = (not with_skip) and (h % 8 == 0)
                        mm(P[pyb][:, (h % 8) * 64:(h % 8 + 1) * 64], Lt[ls][:, j, :], XX[:, 0, h * 64:(h + 1) * 64],
                           first, [ln_, "XX"], pyb)

                if lsfix is None:
                    hb_a(2 * g)
                    yield
                    hb_a(2 * g + 1)
                    yield
                    hb_b(2 * g)
                    yield
                    hb_b(2 * g + 1)
                    yield
                else:
                    for hb in (2 * g, 2 * g + 1):
                        hb_a(hb)
                        yield
                        hb_b(hb)
                        yield
                bn = nextbank()
                mm(P[bn][:], xbcT[:, 10 + g, :], Sbf[:, g * 512:(g + 1) * 512], True, ["xbcT", "Sbf"], bn)
                tt("dve", y1[:, g * 512:(g + 1) * 512].rearrange("p (h d) -> p h d", h=8),
                   P[bn][:].rearrange("p (h d) -> p h d", h=8), bc3(smc(ss, 64 + g * 8, 8), 64),
                   ALU.mult, [bn, n_("ecs")], ["y1"])
                tt("dve", y1[:, g * 512:(g + 1) * 512], y1[:, g * 512:(g + 1) * 512], P[pyb][:], ALU.add,
                   ["y1", pyb], ["y1"])
                yield
            if bwd:
                gg_ = [grp(0, "PY", 0), grp(1, "OA", 1)]
                while gg_:
                    for q_ in list(gg_):
                        try:
                            next(q_)
                        except StopIteration:
                            gg_.remove(q_)
                    yield
            else:
                for g in range(2):
                    yield from grp(g, "PY", None)
        tt("pool", Sst[:].rearrange("p (h d) -> p h d", h=16), Sst[:].rearrange("p (h d) -> p h d", h=16),
           bc3(smc(ss, 96), 64), ALU.mult, ["Sst", n_("etot")], ["Sst"])
        for g in range(2):
            bn = nextbank()
            mm(P[bn][:], Bt[:, g, :], XX[:, 1, g * 512:(g + 1) * 512], True, ["Bt", "XX"], bn)
            tt("dve", Sst[:, g * 512:(g + 1) * 512], Sst[:, g * 512:(g + 1) * 512], P[bn][:], ALU.add,
               ["Sst", bn], ["Sst"])
        cp("dve", Sbf[:], Sst[:], ["Sst"], ["Sbf"])
        yield

    def attention(i, has_prev):
        blocks = ([0] if has_prev else []) + [1, 2]
        obanks = ["OA", "OB", "OC"]

        def oslot(h):
            return obanks[h // 7], (h % 7) * 65

        firstw = {b_: True for b_ in obanks}
        bx = ("PA", "PB")
        rounds = [(gp, o) for gp in range(2) for o in blocks]

        def step_a(r):
            gp, o = rounds[r]
            ks = (i + o - 1) % 3
            for par in range(2):
                bi0 = gp * 8 + par * 4
                mm(P[bx[par]][:], cb[:, B_ID:B_ID + 128],
                   biasT[:, o, bi0:bi0 + 4, :].rearrange("p a b -> p (a b)"), True, ["cb", "biasT"], bx[par])
            for k4 in range(4):
                for par in range(2):
                    h = 8 * gp + 2 * k4 + par
                    g, j = h // 4, h // 2
                    mm(P[bx[par]][:, k4 * 128:(k4 + 1) * 128], KT[ks][par * 64:(par + 1) * 64, g, :],
                       QT[par * 64:(par + 1) * 64, j, :], False, ["KT%d" % ks, "QT"], bx[par])
            act(pT[r % 2][:], PAB[:], AF.Exp, ["PA", "PB"], ["pT%d" % (r % 2)])

        def step_b(r):
            gp, o = rounds[r]
            ks = (i + o - 1) % 3
            pn = "pT%d" % (r % 2)
            for par in range(2):
                for k4 in range(4):
                    h = 8 * gp + 2 * k4 + par
                    ob, oc = oslot(h)
                    c0 = par * 512 + k4 * 128
                    mm(P[ob][:, oc:oc + 65], pT[r % 2][:, c0:c0 + 128], Va[ks][:, h // 4, :],
                       firstw[ob], [pn, "Va%d" % ks], ob)
                    firstw[ob] = False

        step_a(0)
        yield
        for r in range(len(rounds)):
            if r + 1 < len(rounds):
                step_a(r + 1)
                yield
            step_b(r)
            yield

    def attn_epi():
        obanks = ["OA", "OB", "OC"]
        for bi, ob in enumerate(obanks):
            nh = 7 if bi < 2 else 2
            v = P[ob][:, 0:nh * 65].rearrange("p (h d) -> p h d", h=nh)
            tt("dve", sm[:, 352 + bi * 7:352 + bi * 7 + nh].unsqueeze(2), v[:, :, 64:65],
               esink[:, bi * 7:bi * 7 + nh].unsqueeze(2), ALU.add, [ob, "esink"], ["sm_den"])
        S.op("dve", lambda g_: g_.reciprocal(sm[:, 368:384], sm[:, 352:368]), reads=[buf("sm_den")],
             writes=[buf("sm_rden")])
        for bi, ob in enumerate(obanks):
            nh = 7 if bi < 2 else 2
            v = P[ob][:, 0:nh * 65].rearrange("p (h d) -> p h d", h=nh)
            tt("dve", o1[:, bi * 448:bi * 448 + nh * 64].rearrange("p (h d) -> p h d", h=nh), v[:, :, 0:64],
               bc3(sm[:, 368 + bi * 7:368 + bi * 7 + nh], 64), ALU.mult, [ob, "sm_rden"], ["o1"])
        yield
        tt("dve", o1[:], o1[:], gas[:], ALU.mult, ["o1", "gas"], ["o1"])
        sumsq(o1[:], D, 8, ["o1"], "sm_ssa")
        yield
        rstd_of(sm[:, 8:9], D, sm[:, 9:10], ["sm_ssa"], ["sm_rsa"])
        yield
        act(catbf[:, D:2 * D], o1[:], AF.Copy, ["o1", "sm_rsa"], ["catbf"], scale=sm[:, 9:10])
        yield

    def proj_zg(slot):
        hs = HN[slot]
        for (col0, dst, dname) in ((TM_Z, zs, "zs"), (TM_GA, gas, "gas")):
            for hf in range(2):
                bn = nextbank()
                for kc in range(8):
                    mm(P[bn][:], hT[slot][:, kc, :], w_sb[:, kc, col0 + hf * 512:col0 + (hf + 1) * 512], kc == 0,
                       ["w_fwd", hs], bn)
                act(dst[:, hf * 512:(hf + 1) * 512], P[bn][:], AF.Silu, [bn], [dname])
                yield

    def proj_q(slot):
        hs = HN[slot]
        for r in range(2):
            bn = nextbank()
            for c in range(4):
                ch = r * 4 + c
                for kc in range(8):
                    mm(P[bn][:, c * 128:(c + 1) * 128], w_sb[:, kc, FM + 1536 + ch * 128:FM + 1536 + (ch + 1) * 128],
                       hT[slot][:, kc, :], (c == 0 and kc == 0), ["w_fwd", hs], bn)
            cp("dve", QT[:, r * 4:(r + 1) * 4, :].rearrange("p a b -> p (a b)"), P[bn][:], [bn], ["QT"])
            yield

    def ssd_fwd_main(i):
        dma(t2[:], yb_d[i * T:(i + 1) * T, :], ["ybwd_dram"], ["t2"])
        yield from ssd(i % 2, False, False, True)

    def ssd_epi(i):
        tt("dve", y1[:], y1[:], t2[:], ALU.add, ["y1", "t2"], ["y1"])
        tt("dve", y1[:], y1[:], zs[:], ALU.mult, ["y1", "zs"], ["y1"])
        dma(t2[:], x_d[i * T:(i + 1) * T, :], [], ["t2"])
        yield
        for g in range(2):
            sumsq(y1[:, g * 512:(g + 1) * 512], 512, 4 + g, ["y1"], "sm_ssy%d" % g)
        rstd_of(sm[:, 4:6], 512, sm[:, 6:8], ["sm_ssy0", "sm_ssy1"], ["sm_rsy"])
        yield
        for g in range(2):
            act(catbf[:, g * 512:(g + 1) * 512], y1[:, g * 512:(g + 1) * 512], AF.Copy, ["y1", "sm_rsy"], ["catbf"],
                scale=sm[:, 6 + g:7 + g])
        yield

    def after(flags, keys, g):
        while not all(flags.get(k_) for k_ in keys):
            yield
        yield from g

    def seq(*gs):
        for g in gs:
            if g is not None:
                yield from g

    def par(*gs):
        gens = [g for g in gs if g is not None]
        while gens:
            for g in list(gens):
                try:
                    next(g)
                except StopIteration:
                    gens.remove(g)
            yield

    def epi_chain(i, extra=None, flags=None):
        yield from par(ssd_epi(i), attn_epi())
        for _ in range(int(os.environ.get("KWARM_TAIL", "12"))):
            mm(P["OC"][:, 256:512], cb[:, B_ID:B_ID + 128], cb[:, B_MNF:B_MNF + 256], False, ["cb"], "OC")
        yield from par(tail_H(i, flags), extra)

    def tail_H(i, flags=None):
        catT = XX[:].rearrange("p a (c t) -> p (a c) t", t=T)
        for r in range(2):
            for c in range(8):
                tr(PT[:, c * 128:(c + 1) * 128], catbf[:, (r * 8 + c) * 128:(r * 8 + c + 1) * 128], ["catbf"], "PT")
            cp("act" if r else "dve", catT[:, r * 8:(r + 1) * 8, :].rearrange("p a b -> p (a b)"), PT[:], ["PT"],
               ["XX"])
            yield
        for hf in range(2):
            bn = nextbank()
            for kc in range(16):
                mm(P[bn][:], catT[:, kc, :], wo_sb[:, kc, hf * 512:(hf + 1) * 512], kc == 0, ["XX", "wo_sb"], bn)
            tt("dve", t2[:, hf * 512:(hf + 1) * 512], t2[:, hf * 512:(hf + 1) * 512], P[bn][:], ALU.add,
               ["t2", bn], ["t2"])
            yield
        if flags is not None:
            flags["op"] = True
        sumsq(t2[:], D, 10, ["t2"], "sm_sso")
        rstd_of(sm[:, 10:11], D, sm[:, 11:12], ["sm_sso"], ["sm_rso"])
        S.op("dve", lambda g_: g_.scalar_tensor_tensor(out=y1[:], in0=t2[:], scalar=sm[:, 11:12],
                                                       in1=cf[:, C_FG:C_FG + D], op0=ALU.mult, op1=ALU.mult),
             reads=[buf("t2"), buf("sm_rso"), buf("cf")], writes=[buf("y1")])
        dma(out_d[i * T:(i + 1) * T, :], y1[:], ["y1"], ["out_dram"])
        yield

    def ssd_bwd_chain(i, states_only):
        yield from ssd(i % 2, True, states_only, False)
        if not states_only:
            dma(yb_d[i * T:(i + 1) * T, :], y1[:], ["y1"], ["ybwd_dram"])
        yield

    _run([front(NT - 1, (NT - 1) % 3, False)])
    zero_halo((NT - 1) % 3, left=False)
    _run([front(NT - 2, (NT - 2) % 3, False), dt_and_decay((NT - 1) % 3, 16, True, ss=(NT - 1) % 2)])
    halo((NT - 1) % 3, (NT - 2) % 3, True)
    _run([front(NT - 3, (NT - 3) % 3, False)])
    halo((NT - 2) % 3, (NT - 3) % 3, True)
    bg = bg_setup()
    for i in reversed(range(NT)):
        slot = i % 3
        so = (i >= NTH)
        conv_silu(slot, 10 if so else 12)
        if not so:
            dma(xbc_d[i], xbcT[:].rearrange("p a b -> p (a b)"), ["xbcT"], ["xbc_dram"])
        gens = []
        if i - 3 >= 0:
            gens.append(front(i - 3, slot, False))
        if i - 1 >= 0:
            gens.append(dt_and_decay((i - 1) % 3, 16, True, ss=(i - 1) % 2))
        if so:
            gens.append(ssd_bwd_chain(i, so))
        else:
            gens.insert(0, ssd_bwd_chain(i, so))
        rv = os.environ.get("KWARM_REV", "0")
        S.warm_fn = warm if (NWARM and (rv == "1" or (rv == "own" and not so) or (rv == "far" and so))) else None
        _run(gens, bg if so else None)
        S.warm_fn = None
        if i == NTH:
            _run([bg])
        if i - 3 >= 0:
            halo((i - 2) % 3, slot, True)
            if i - 3 == 0:
                zero_halo(slot, left=True)

    S.op("pool", lambda g: g.memset(Sst[:], 0.0), writes=[buf("Sst")])
    S.op("pool", lambda g: g.memset(Sbf[:], 0.0), writes=[buf("Sbf")])
    def load_xbc(i):
        dma(xbcT[:].rearrange("p a b -> p (a b)"), xbc_d[i], ["xbc_dram"], ["xbcT"])

    def front_kvonly(i, slot):
        yield from front(i, slot, True, head_only=True)
        yield from front_kv(i, slot)

    _run([front_kvonly(0, 0)])
    load_xbc(0)
    _run([front_kvonly(1, 1), dt_and_decay(0, 0, False), proj_q(0)])
    _run([proj_zg(0)])
    def one_step(g):
        next(g)
        yield

    g_ssd = ssd_fwd_main(0)
    for i in range(NTH):
        slot = i % 2
        gens = [g_ssd, seq(attention(i, i > 0),
                           proj_q(1 - slot) if i + 1 < NTH else None,
                           front_kv(i + 2, slot) if i + 2 <= NTH else None)]
        if i + 1 < NTH:
            gens.append(dt_and_decay(1 - slot, 0, False))
        if i + 2 <= NTH:
            gens.append(front(i + 2, slot, True, head_only=True))
        S.warm_fn = warm if NWARM else None
        _run(gens)
        S.warm_fn = None
        if i + 1 < NTH:
            load_xbc(i + 1)
        fl = {}
        gens = [epi_chain(i, proj_zg(1 - slot) if i + 1 < NTH else None, fl)]
        if i + 1 < NTH:
            g_ssd = ssd_fwd_main(i + 1)
            gens.append(after(fl, ["op"], one_step(g_ssd)))
        _run(gens)
    S.finish()
    return nc


def _t5_bucket_table():
    import jax
    import jax.numpy as jnp
    with jax.default_device(jax.devices("cpu")[0]):
        return _t5_bucket_table_impl(jnp)


def _t5_bucket_table_impl(jnp):
    rel = jnp.arange(-128, 129)
    half = 16
    ret = (rel > 0).astype(jnp.int32) * half
    n = jnp.abs(rel)
    is_small = n < 8
    nf = jnp.maximum(n, 1).astype(jnp.float32)
    large = 8 + (jnp.log(nf / 8) / math.log(128 / 8) * (half - 8)).astype(jnp.int32)
    large = jnp.minimum(large, half - 1)
    return np.asarray(ret + jnp.where(is_small, n, large))


def _consts_bf():
    idx = np.arange(128)
    c = np.zeros((128, NCB), np.float32)
    c[:, B_ID:B_ID + 128] = np.eye(128)
    c[:, B_TRF:B_TRF + 128] = (idx[:, None] <= idx[None, :])
    c[:, B_TRB:B_TRB + 128] = (idx[:, None] >= idx[None, :])
    c[:, B_ONE:B_ONE + 128] = 1.0
    c[:, B_MNF:B_MNF + 512] = np.tile(np.where(idx[:, None] > idx[None, :], NEG, 0.0), (1, 4))
    c[:, B_MNB:B_MNB + 512] = np.tile(np.where(idx[:, None] < idx[None, :], NEG, 0.0), (1, 4))
    return c.astype(ml_dtypes.bfloat16)


def _prep_core(inp, b, half, NTH, bucket):
    f32 = np.float32
    x = inp["x"][b]
    if half:
        x = x[::-1]
    x = np.ascontiguousarray(x, dtype=f32)
    w_in = np.asarray(inp["w_in"], f32)
    z, xbc, dtc = w_in[:, 0:1024], w_in[:, 1024:2560], w_in[:, 2560:2592]
    q, k, v, ga = w_in[:, 2592:3616], w_in[:, 3616:3872], w_in[:, 3872:4128], w_in[:, 4128:5152]
    d0, d1 = dtc[:, 0:16], dtc[:, 16:32]
    if half:
        d0, d1 = d1, d0
    kd = np.concatenate([np.concatenate([k[:, i * 64:(i + 1) * 64]] * 2, axis=1) for i in range(4)], axis=1)
    w = np.ascontiguousarray(np.concatenate([z, ga, v, d0, d1, xbc, q, kd], axis=1))
    assert w.shape[1] == NCOLS
    idx = np.arange(128)
    cfa = np.zeros((128, NCF), f32)
    cfa[:, C_M01F:C_M01F + 128] = (idx[:, None] <= idx[None, :])
    cfa[:, C_M01B:C_M01B + 128] = (idx[:, None] >= idx[None, :])
    order = [1, 0] if half else [0, 1]
    cfa[:, C_DTB:C_DTB + 32] = np.asarray(inp["dt_bias"], f32)[order].reshape(1, 32)
    cfa[:, C_ALOG:C_ALOG + 32] = np.asarray(inp["a_log"], f32)[order].reshape(1, 32)
    cfa[:, C_SINK:C_SINK + 16] = np.asarray(inp["sink"], f32).reshape(1, 16)
    cfa[:, C_FG:C_FG + D] = np.asarray(inp["final_norm_g"], f32).reshape(1, D)
    cfa[:, C_GIN:C_GIN + 8] = np.asarray(inp["norm_in_g"], f32).reshape(8, 128).T
    cfa[:, C_GSSD:C_GSSD + 8] = np.asarray(inp["ssd_norm_g"], f32).reshape(8, 128).T
    cfa[:, C_GATT:C_GATT + 8] = np.asarray(inp["attn_norm_g"], f32).reshape(8, 128).T
    cfa[:, C_CB:C_CB + 12] = np.asarray(inp["conv_b"], f32).reshape(12, 128).T
    cw = np.asarray(inp["conv_w"], f32)
    if half:
        cw = cw[::-1]
    cfa[:, C_CW:C_CW + 60] = cw.reshape(5, 12, 128).transpose(2, 1, 0).reshape(128, 60)
    cfa[:, C_DSK:C_DSK + 8] = np.repeat(np.asarray(inp["d_skip"], f32), 64).reshape(8, 128).T
    cfa[:, C_NH] = -0.5
    rb = np.asarray(inp["rel_bias"], f32)
    tq = idx[:, None] - idx[None, :]
    bt = np.full((128, 3, 16, 128), NEG, f32)
    for o in range(3):
        rl = (o - 1) * 128 + tq
        valid = np.abs(rl) <= 128
        rg = -rl if half else rl
        bidx = bucket[np.clip(rg, -128, 128) + 128]
        vals = rb[bidx]
        hperm = [8 * gp + 2 * k4 + par for gp in range(2) for par in range(2) for k4 in range(4)]
        bt[:, o] = np.where(valid[:, None, :], vals.transpose(0, 2, 1)[:, hperm, :], NEG)
    return {
        "x": x, "w_in": w, "w_out": np.ascontiguousarray(np.asarray(inp["w_out"], f32)[0]),
        "cst_f32": cfa, "cst_bf": _consts_bf(),
        "biasT": np.ascontiguousarray(bt.reshape(128, -1)).astype(ml_dtypes.bfloat16),
    }


_NC_CACHE = {}


def kernel(**inputs):
    NTH = NTH_FULL
    x = np.asarray(inputs["x"])
    Bsz, Sq, _ = x.shape
    assert Sq == 2 * NTH * T and Bsz == 4
    bucket = _t5_bucket_table()
    in_maps = [_prep_core(inputs, c // 2, c % 2, NTH, bucket) for c in range(8)]
    if NTH not in _NC_CACHE:
        _NC_CACHE[NTH] = _build(NTH)
    nc = _NC_CACHE[NTH]
    res = run_bass_kernel_spmd(nc, in_maps, core_ids=list(range(8)))
    out = np.empty((Bsz, Sq, D), np.float32)
    h = NTH * T
    for c in range(8):
        o = np.asarray(res.results[c]["out"], np.float32)
        if c % 2 == 0:
            out[c // 2, 0:h] = o
        else:
            out[c // 2, h:] = o[::-1]
    return out
```

```python
import math
import os
import numpy as np
import ml_dtypes
import concourse.bass as bass
import concourse.mybir as mybir
from concourse.bass_utils import run_bass_kernel_spmd

F32 = mybir.dt.float32
BF16 = mybir.dt.bfloat16
AF = mybir.ActivationFunctionType
ALU = mybir.AluOpType

T = 128
D = 1024
NTH_FULL = 32
TM_Z, TM_GA, TM_V, TM_DT, FM = 0, 1024, 2048, 2304, 2336
NCOLS = 5408
EPS = 1e-6
NEG = -30000.0

C_M01F, C_M01B, C_DTB, C_ALOG, C_SINK, C_FG = 0, 128, 256, 288, 320, 336
C_GIN, C_GSSD, C_GATT, C_CB, C_CW, C_DSK, C_NH, NCF = 1360, 1368, 1376, 1384, 1396, 1456, 1464, 1472
B_ID, B_TRF, B_TRB, B_ONE, B_MNF, B_MNB, NCB = 0, 128, 256, 384, 512, 1024, 1536


class _Buf:
    __slots__ = ("name", "last_w", "readers", "excl")

    def __init__(self, name, excl=False):
        self.name = name
        self.last_w = None
        self.readers = []
        self.excl = excl


class _Sched:
    NDMA = 8

    def __init__(self, nc):
        self.nc = nc
        self.eng = {"pe": nc.tensor, "act": nc.scalar, "dve": nc.vector, "pool": nc.gpsimd, "sp": nc.sync}
        names = list(self.eng) + ["dma%d" % j for j in range(self.NDMA)]
        self.sem = {n: nc.alloc_semaphore("s_" + n) for n in names}
        self.cnt = {n: 0 for n in names}
        self.inc = {n: (16 if n.startswith("dma") else 1) for n in names}
        self.seen = {e: {n: 0 for n in names} for e in self.eng}
        self.rr = 0

    def _deps(self, e, reads, writes):
        deps = {}

        def need(dep, raw):
            if dep is None:
                return
            f, idx = dep
            if f == e and (e == "pe" or not raw):
                return
            if deps.get(f, 0) < idx:
                deps[f] = idx

        for b in reads:
            need(b.last_w, True)
            if b.excl:
                for r in b.readers:
                    need(r, False)
        for b in writes:
            need(b.last_w, False)
            for r in b.readers:
                need(r, False)
        return deps

    warm_fn = None
    _in_warm = False

    def _wait(self, e, deps):
        eng = self.eng[e]
        if e == "pe" and self.warm_fn is not None and not self._in_warm:
            if any(self.seen[e][f] < idx for f, idx in deps.items()):
                self._in_warm = True
                self.warm_fn()
                self._in_warm = False
        for f, idx in deps.items():
            if self.seen[e][f] >= idx:
                continue
            eng.wait_ge(self.sem[f], idx * self.inc[f])
            self.seen[e][f] = idx

    def _mark(self, me, reads, writes):
        for b in writes:
            b.last_w = me
            b.readers = []
        for b in reads:
            if b not in writes:
                b.readers.append(me)

    def op(self, e, fn, reads=(), writes=()):
        self._wait(e, self._deps(e, reads, writes))
        inst = fn(self.eng[e])
        self.cnt[e] += 1
        inst.then_inc(self.sem[e], 1)
        self._mark((e, self.cnt[e]), reads, writes)

    def dma(self, out, in_, reads=(), writes=()):
        d = "dma%d" % self.rr
        self.rr = (self.rr + 1) % self.NDMA
        deps = self._deps("sp", reads, writes)
        if self.cnt[d] > 0:
            deps[d] = max(deps.get(d, 0), self.cnt[d])
        self._wait("sp", deps)
        inst = self.nc.sync.dma_start(out=out, in_=in_)
        self.cnt[d] += 1
        inst.then_inc(self.sem[d], 16)
        self._mark((d, self.cnt[d]), reads, writes)

    def finish(self):
        deps = {n: c for n, c in self.cnt.items() if c > 0 and n != "sp"}
        self._wait("sp", deps)


def _run(gens, bg=None):
    gens = [g for g in gens if g is not None]
    while gens:
        for g in list(gens):
            try:
                next(g)
            except StopIteration:
                gens.remove(g)
        if bg is not None:
            try:
                next(bg)
            except StopIteration:
                bg = None


def _build(NTH):
    NT = 2 * NTH
    nc = bass.Bass("TRN2", target_bir_lowering=False, dynamic_dma_scratch_size=512)
    x_d = nc.dram_tensor("x", [NT * T, D], F32, kind="ExternalInput").ap()
    w_d = nc.dram_tensor("w_in", [D, NCOLS], F32, kind="ExternalInput").ap()
    wo_d = nc.dram_tensor("w_out", [2 * D, D], F32, kind="ExternalInput").ap()
    cf_d = nc.dram_tensor("cst_f32", [128, NCF], F32, kind="ExternalInput").ap()
    cb_d = nc.dram_tensor("cst_bf", [128, NCB], BF16, kind="ExternalInput").ap()
    bt_d = nc.dram_tensor("biasT", [128, 3 * 16 * 128], BF16, kind="ExternalInput").ap()
    out_d = nc.dram_tensor("out", [NTH * T, D], F32, kind="ExternalOutput").ap()
    yb_d = nc.dram_tensor("ybwd", [NTH * T, D], F32, kind="Internal").ap()
    xbc_d = nc.dram_tensor("xbcs", [NTH, 128, 1536], BF16, kind="Internal").ap()

    def sb(name, shape, dt):
        return nc.alloc_sbuf_tensor(name, shape, dt)

    w_sb = sb("w_sb", [128, 8, NCOLS], BF16)
    wo_sb = sb("wo_sb", [128, 16, D], BF16)
    biasT = sb("biasT_sb", [128, 3, 16, 128], BF16)
    dconv = sb("dconv", [128, 60, 128], BF16)
    dskd = sb("dskd", [128, 8, 128], BF16)
    cf = sb("cf", [128, NCF], F32)
    cb = sb("cb", [128, NCB], BF16)
    xf = sb("xf", [128, D], F32)
    Sst = sb("Sst", [128, D], F32)
    y1 = sb("y1", [128, D], F32)
    t2 = sb("t2", [128, D], F32)
    o1 = sb("o1", [128, D], F32)
    hbf = sb("hbf", [128, D], BF16)
    jk = sb("jk", [128, D], BF16)
    hT = [sb("hT%d" % i, [128, 8, T], BF16) for i in range(2)]
    xpad = [sb("xpad%d" % i, [128, 12, 132], BF16) for i in range(2)]
    xbcT = sb("xbcT", [128, 12, T], BF16)
    QT = sb("QT", [128, 8, T], BF16)
    KT = [sb("KT%d" % i, [128, 4, T], BF16) for i in range(3)]
    Va = [sb("Va%d" % i, [128, 4, 65], BF16) for i in range(3)]
    zs = sb("zs", [128, D], BF16)
    gas = sb("gas", [128, D], BF16)
    XX = sb("XX", [128, 2, D], BF16)
    Bt = sb("Bt", [128, 2, T], BF16)
    Lt = [sb("Lt%d" % i, [128, 4, T], BF16) for i in range(2)]
    CBm = sb("CBm", [128, 2, T], F32)
    Sbf = sb("Sbf", [128, D], BF16)
    pT = [sb("pT%d" % i, [128, 1024], BF16) for i in range(2)]
    catbf = sb("catbf", [128, 2 * D], BF16)
    sm = sb("sm", [128, 384], F32)
    smb = sb("smb", [128, 64], BF16)
    A_bc = sb("A_bc", [128, 32], F32)
    esink = sb("esink", [128, 16], F32)

    HN = ["hT0", "hT1", "QT"]
    XN = ["xpad0", "xpad1", "catbf"]
    hT.append(QT)
    xpad.append(catbf[:, 0:12 * 132].rearrange("p (a b) -> p a b", a=12))

    PAB = nc.alloc_psum_tensor("PAB", [128, 1024], F32)
    P = {n: nc.alloc_psum_tensor(n, [128, 512], F32) for n in ["PS", "PY", "OA", "OB", "OC"]}
    P["PA"] = PAB[:, 0:512]
    P["PB"] = PAB[:, 512:1024]
    PT = nc.alloc_psum_tensor("PT", [128, 1024], BF16)

    S = _Sched(nc)
    B = {}

    def buf(name, excl=False):
        if name not in B:
            B[name] = _Buf(name, excl)
        return B[name]

    for n in list(P) + ["PT"]:
        buf(n, True)

    def bk(n):
        return B[n]

    def mm(outap, lhsT, rhs, start, reads, bank):
        S.op("pe", lambda e: e.matmul(outap, lhsT, rhs, start=start, stop=True, skip_group_check=True),
             reads=[buf(r) for r in reads], writes=[bk(bank)])

    def tr(outap, inap, reads, bank):
        S.op("pe", lambda e: e.transpose(outap, inap, cb[:, B_ID:B_ID + 128]),
             reads=[buf(r) for r in reads] + [buf("cb")], writes=[bk(bank)])

    def act(outap, inap, func, reads, writes, **kw):
        S.op("act", lambda e: e.activation(outap, inap, func, **kw),
             reads=[buf(r) for r in reads], writes=[buf(w) for w in writes])

    def tt(e, outap, a, b_, op, reads, writes):
        S.op(e, lambda g: g.tensor_tensor(outap, a, b_, op),
             reads=[buf(r) for r in reads], writes=[buf(w) for w in writes])

    def ts(e, outap, a, s1, s2, op0, op1, reads, writes):
        S.op(e, lambda g: g.tensor_scalar(outap, a, s1, s2, op0, op1),
             reads=[buf(r) for r in reads], writes=[buf(w) for w in writes])

    def cp(e, outap, inap, reads, writes):
        if e == "act":
            S.op(e, lambda g: g.activation(outap, inap, AF.Copy),
                 reads=[buf(r) for r in reads], writes=[buf(w) for w in writes])
        else:
            S.op(e, lambda g: g.tensor_copy(outap, inap),
                 reads=[buf(r) for r in reads], writes=[buf(w) for w in writes])

    def scale_cast(e, outap, inap, col, extra, reads, writes):
        if e == "act":
            if extra != 1.0:
                raise ValueError
            act(outap, inap, AF.Copy, reads + ["cf"], writes, scale=cf[:, col:col + 1])
        elif extra != 1.0:
            ts(e, outap, inap, cf[:, col:col + 1], extra, ALU.mult, ALU.mult, reads + ["cf"], writes)
        else:
            ts(e, outap, inap, cf[:, col:col + 1], None, ALU.mult, ALU.bypass, reads + ["cf"], writes)

    def dma(outap, inap, reads, writes):
        S.dma(outap, inap, reads=[buf(r) for r in reads], writes=[buf(w) for w in writes])

    def bc3(ap2, n):
        return ap2.unsqueeze(2).to_broadcast([128, ap2.shape[1], n])

    NWARM = int(os.environ.get("KWARM", "1"))

    def warm():
        for _ in range(NWARM):
            mm(P["OC"][:, 256:512], cb[:, B_ID:B_ID + 128], cb[:, B_MNF:B_MNF + 256], False, ["cb"], "OC")

    def wname(c0):
        return "w_rev" if 2304 <= c0 < 3872 else "w_fwd"

    dma(cf[:], cf_d, [], ["cf"])
    dma(cb[:], cb_d, [], ["cb"])
    q0, q1 = FM + 1536, FM + 2560
    kq = [0]

    def load_w(kc, c0, w, st, sn):
        dma(st[:, 0:w], w_d[kc * 128:(kc + 1) * 128, c0:c0 + w], [], [sn])
        isq = (q0 <= c0 < q1)
        assert isq == (q0 <= c0 + w - 1 < q1)
        kq[0] += 1
        e = "dve" if (isq or kq[0] % 2) else "act"
        scale_cast(e, w_sb[:, kc, c0:c0 + w], st[:, 0:w], C_GIN + kc, 0.125 if isq else 1.0, [sn], [wname(c0)])

    stg = [(xf, "xf"), (y1, "y1"), (t2, "t2"), (o1, "o1")]
    k = 0
    for kc in range(8):
        for (c0, w) in [(2304, 784), (3088, 784)]:
            st, sn = stg[k % 4]
            k += 1
            load_w(kc, c0, w, st, sn)
    for c in range(12):
        for kk in range(5):
            j = c * 5 + kk
            scale_cast("dve" if j % 2 else "act", dconv[:, j, :], cb[:, B_ID:B_ID + 128], C_CW + j, 1.0,
                       ["cb"], ["dconv"])
    for c in range(8):
        scale_cast("dve", dskd[:, c, :], cb[:, B_ID:B_ID + 128], C_DSK + c, 1.0, ["cb"], ["dskd"])
    act(A_bc[:], cf[:, C_ALOG:C_ALOG + 32], AF.Exp, ["cf"], ["A_bc"])
    ts("dve", A_bc[:], A_bc[:], -1.0, None, ALU.mult, ALU.bypass, ["A_bc"], ["A_bc"])
    act(esink[:], cf[:, C_SINK:C_SINK + 16], AF.Exp, ["cf"], ["esink"])
    for i in range(3):
        S.op("pool", lambda g, i=i: g.memset(Va[i][:, :, 64:65], 1.0), writes=[buf("Va%d" % i)])
    for i in range(2):
        S.op("pool", lambda g, i=i: g.memset(xpad[i][:], 0.0), writes=[buf("xpad%d" % i)])
    S.op("pool", lambda g: g.memset(Sst[:], 0.0), writes=[buf("Sst")])
    S.op("pool", lambda g: g.memset(Sbf[:], 0.0), writes=[buf("Sbf")])
    S.op("pool", lambda g: g.memset(catbf[:], 0.0), writes=[buf("catbf")])
    S.op("dve", lambda g: g.memset(P["OC"][:], 0.0), writes=[bk("OC")])

    def bg_setup():
        st3 = [(y1, "y1"), (t2, "t2"), (o1, "o1")]
        kk_ = 0
        dma(biasT[:].rearrange("p a b c -> p (a b c)"), bt_d, [], ["biasT"])
        yield
        cols = [(0, 1024), (1024, 1024), (2048, 256), (3872, 1024), (4896, 512)]
        for kc in range(8):
            for (c0, w) in cols:
                st, sn = st3[kk_ % 3]
                kk_ += 1
                load_w(kc, c0, w, st, sn)
                yield
        for kc in range(16):
            st, sn = st3[kk_ % 3]
            kk_ += 1
            dma(st[:], wo_d[kc * 128:(kc + 1) * 128, :], [], [sn])
            gcol = (C_GSSD + kc) if kc < 8 else (C_GATT + kc - 8)
            scale_cast("dve" if kk_ % 2 else "act", wo_sb[:, kc, :], st[:], gcol, 1.0, [sn], ["wo_sb"])
            yield

    rot = [0]

    def nextbank():
        rot[0] ^= 1
        return "PA" if rot[0] else "PB"

    def rstd_of(ss_ap, n, out_ap, rd, wr, eps=EPS):
        ts("dve", out_ap, ss_ap, 1.0 / n, eps, ALU.mult, ALU.add, rd, wr)
        k_ = out_ap.shape[1]
        tt("pool", out_ap, out_ap, cf[:, C_NH:C_NH + 1].to_broadcast([128, k_]), ALU.pow, wr + ["cf"], wr)

    def sumsq(src_ap, n_el, out_col, rd, wr):
        act(jk[:, 0:n_el], src_ap, AF.Square, rd + ["jk"], ["jk", wr], accum_out=sm[:, out_col:out_col + 1])

    def front(i, slot, want_kv, head_only=False):
        hs = HN[slot]
        dma(xf[:], x_d[i * T:(i + 1) * T, :], [], ["xf"])
        sumsq(xf[:], D, 0, ["xf"], "sm_ss")
        rstd_of(sm[:, 0:1], D, sm[:, 1:2], ["sm_ss"], ["sm_rs"])
        act(hbf[:], xf[:], AF.Copy, ["xf", "sm_rs"], ["hbf"], scale=sm[:, 1:2])
        yield
        if head_only == "a":
            return
        yield from front_b(i, slot, want_kv, head_only)

    def front_b(i, slot, want_kv, head_only=False):
        hs = HN[slot]
        for kc in range(8):
            tr(PT[:, kc * 128:(kc + 1) * 128], hbf[:, kc * 128:(kc + 1) * 128], ["hbf"], "PT")
        cp("dve", hT[slot][:].rearrange("p a b -> p (a b)"), PT[:], ["PT"], [hs])
        yield
        if head_only:
            return
        yield from front_tail(i, slot, want_kv)

    def front_tail(i, slot, want_kv):
        hs = HN[slot]
        xs_ = XN[slot]
        nch = 10 if i > NTH else 12
        for r in range(3):
            bn = nextbank()
            ncr = min(4, nch - r * 4)
            for c in range(ncr):
                ch = r * 4 + c
                for kc in range(8):
                    mm(P[bn][:, c * 128:(c + 1) * 128], w_sb[:, kc, FM + ch * 128:FM + (ch + 1) * 128],
                       hT[slot][:, kc, :], (c == 0 and kc == 0), ["w_rev", hs], bn)
            cp("act" if r == 1 else "dve", xpad[slot][:, r * 4:r * 4 + ncr, 2:130],
               P[bn][:, 0:ncr * 128].rearrange("p (a b) -> p a b", a=ncr), [bn], [xs_])
            yield
        if want_kv:
            yield from front_kv(i, slot)

    def front_kv(i, slot):
        hs = HN[slot]
        if True:
            ks = i % 3
            bn = nextbank()
            for c in range(4):
                for kc in range(8):
                    mm(P[bn][:, c * 128:(c + 1) * 128], w_sb[:, kc, FM + 2560 + c * 128:FM + 2560 + (c + 1) * 128],
                       hT[slot][:, kc, :], (c == 0 and kc == 0), ["w_fwd", hs], bn)
            cp("act", KT[ks][:].rearrange("p a b -> p (a b)"), P[bn][:], [bn], ["KT%d" % ks])
            yield
            bn = nextbank()
            for kc in range(8):
                mm(P[bn][:, 0:256], hT[slot][:, kc, :], w_sb[:, kc, TM_V:TM_V + 256], kc == 0, ["w_fwd", hs], bn)
            cp("dve", Va[ks][:, :, 0:64], P[bn][:, 0:256].rearrange("p (a b) -> p a b", a=4), [bn], ["Va%d" % ks])
            yield

    def halo(cur, nxt, nxt_precedes):
        a, b_ = XN[cur], XN[nxt]
        if nxt_precedes:
            cp("pool", xpad[cur][:, :, 0:2], xpad[nxt][:, :, 128:130], [b_], [a])
            cp("pool", xpad[nxt][:, :, 130:132], xpad[cur][:, :, 2:4], [a], [b_])
        else:
            cp("pool", xpad[cur][:, :, 130:132], xpad[nxt][:, :, 2:4], [b_], [a])
            cp("pool", xpad[nxt][:, :, 0:2], xpad[cur][:, :, 128:130], [a], [b_])

    def zero_halo(slot, left):
        sl = slice(0, 2) if left else slice(130, 132)
        S.op("pool", lambda g: g.memset(xpad[slot][:, :, sl], 0.0), writes=[buf(XN[slot])])

    def conv_silu(slot, nchunks):
        xs_ = XN[slot]
        chunks = list(range(nchunks))
        for r0 in range(0, len(chunks), 4):
            grp = chunks[r0:r0 + 4]
            bn = nextbank()
            for ci, ch in enumerate(grp):
                for kk in range(5):
                    mm(P[bn][:, ci * 128:(ci + 1) * 128], dconv[:, ch * 5 + kk, :], xpad[slot][:, ch, kk:kk + 128],
                       (ci == 0 and kk == 0), ["dconv", xs_], bn)
            for ci, ch in enumerate(grp):
                act(xbcT[:, ch, :], P[bn][:, ci * 128:(ci + 1) * 128], AF.Silu, [bn, "cf"], ["xbcT"],
                    bias=cf[:, C_CB + ch:C_CB + ch + 1])

    def smc(ss, off, n=16):
        b0 = 32 + ss * 160 + off
        return sm[:, b0:b0 + n]

    def dt_and_decay(slot, dcol, bwd, ss=None):
        ss = slot if ss is None else ss
        hs = HN[slot]
        n_ = lambda t_: "sm_%s%d" % (t_, ss)
        sb_ = "smb%d" % ss
        ahi, alo = smb[:, ss * 32:ss * 32 + 16], smb[:, ss * 32 + 16:ss * 32 + 32]
        for kc in range(8):
            mm(P["PS"][:, 0:16], hT[slot][:, kc, :], w_sb[:, kc, TM_DT + dcol:TM_DT + dcol + 16], kc == 0,
               ["w_rev", hs], "PS")
        tt("dve", smc(ss, 128), P["PS"][:, 0:16], cf[:, C_DTB + dcol:C_DTB + dcol + 16], ALU.add,
           ["PS", "cf"], [n_("t")])
        act(smc(ss, 128), smc(ss, 128), AF.Exp, [n_("t")], [n_("t")])
        act(smc(ss, 0), smc(ss, 128), AF.Ln, [n_("t")], [n_("dt")], bias=1.0)
        yield
        tt("dve", smc(ss, 16), smc(ss, 0), A_bc[:, dcol:dcol + 16], ALU.mult, [n_("dt"), "A_bc"], [n_("a")])
        cp("dve", ahi, smc(ss, 16), [n_("a")], [sb_])
        tt("dve", smc(ss, 128), smc(ss, 16), ahi, ALU.subtract, [n_("a"), sb_], [n_("t")])
        cp("dve", alo, smc(ss, 128), [n_("t")], [sb_])
        yield
        tri = cb[:, B_TRB:B_TRB + 128] if bwd else cb[:, B_TRF:B_TRF + 128]
        mm(P["PS"][:, 0:16], tri, ahi, True, ["cb", sb_], "PS")
        mm(P["PS"][:, 0:16], tri, alo, False, ["cb", sb_], "PS")
        mm(P["PS"][:, 16:32], cb[:, B_ONE:B_ONE + 128], ahi, False, ["cb", sb_], "PS")
        mm(P["PS"][:, 16:32], cb[:, B_ONE:B_ONE + 128], alo, False, ["cb", sb_], "PS")
        cp("dve", smc(ss, 32), P["PS"][:, 0:16], ["PS"], [n_("cs")])
        ts("dve", smc(ss, 48), P["PS"][:, 0:16], -1.0, None, ALU.mult, ALU.bypass, ["PS"], [n_("ncs")])
        tt("dve", smc(ss, 128), P["PS"][:, 16:32], smc(ss, 32), ALU.subtract, ["PS", n_("cs")], [n_("t")])
        act(smc(ss, 64), P["PS"][:, 0:16], AF.Exp, ["PS"], [n_("ecs")])
        act(smc(ss, 96), P["PS"][:, 16:32], AF.Exp, ["PS"], [n_("etot")])
        act(smc(ss, 80), smc(ss, 128), AF.Exp, [n_("t")], [n_("dec")])
        yield
        tt("dve", smc(ss, 112), smc(ss, 0), smc(ss, 80), ALU.mult, [n_("dt"), n_("dec")], [n_("dtdec")])
        yield

    def ssd(ss, bwd, states_only, with_skip):
        n_ = lambda t_: "sm_%s%d" % (t_, ss)
        sb_ = "smb%d" % ss
        for c in range(8):
            tr(PT[:, c * 128:(c + 1) * 128], xbcT[:, c, :], ["xbcT"], "PT")
        ptv = PT[:].rearrange("p (h d) -> p h d", h=16)
        if not states_only:
            tt("dve", XX[:, 0, :].rearrange("p (h d) -> p h d", h=16), ptv, bc3(smc(ss, 0), 64), ALU.mult,
               ["PT", n_("dt")], ["XX"])
        tt("dve", XX[:, 1, :].rearrange("p (h d) -> p h d", h=16), ptv, bc3(smc(ss, 112), 64), ALU.mult,
           ["PT", n_("dtdec")], ["XX"])
        for g in range(2):
            tr(PT[:, g * 128:(g + 1) * 128], xbcT[:, 8 + g, :], ["xbcT"], "PT")
        cp("dve", Bt[:].rearrange("p a b -> p (a b)"), PT[:, 0:256], ["PT"], ["Bt"])
        yield
        if not states_only:
            for g in range(2):
                mm(P["PS"][:, 128 + g * 128:128 + (g + 1) * 128], xbcT[:, 8 + g, :], xbcT[:, 10 + g, :],
                   g == 0, ["xbcT"], "PS")
            mcol = C_M01B if bwd else C_M01F
            tt("dve", CBm[:], P["PS"][:, 128:384].rearrange("p (a b) -> p a b", a=2),
               cf[:, mcol:mcol + 128].unsqueeze(1).to_broadcast([128, 2, 128]), ALU.mult, ["PS", "cf"], ["CBm"])
            yield
            mn = B_MNB if bwd else B_MNF
            tri = cb[:, B_TRB:B_TRB + 128] if bwd else cb[:, B_TRF:B_TRF + 128]
            def grp(g, pyb, lsfix):
                if with_skip:
                    for c in range(4):
                        mm(P[pyb][:, c * 128:(c + 1) * 128], xbcT[:, g * 4 + c, :], dskd[:, g * 4 + c, :],
                           c == 0, ["xbcT", "dskd"], pyb)
                def hb_a(hb):
                    bn = nextbank()
                    mm(P[bn][:], cb[:, B_ID:B_ID + 128], cb[:, mn:mn + 512], True, ["cb"], bn)
                    for j in range(4):
                        h = hb * 4 + j
                        for part in range(2):
                            col = ss * 32 + part * 16 + h
                            mm(P[bn][:, j * 128:(j + 1) * 128], smb[:, col:col + 1].to_broadcast([128, 128]), tri,
                               False, ["cb", sb_], bn)
                    ls = (hb % 2) if lsfix is None else lsfix
                    ln_ = "Lt%d" % ls
                    for j in range(4):
                        h = hb * 4 + j
                        act(Lt[ls][:, j, :], P[bn][:, j * 128:(j + 1) * 128], AF.Exp, [bn, n_("ncs")], [ln_],
                            bias=smc(ss, 48 + h, 1))
                    tt("dve", Lt[ls][:], Lt[ls][:], CBm[:, g:g + 1, :].to_broadcast([128, 4, 128]), ALU.mult,
                       [ln_, "CBm"], [ln_])

                def hb_b(hb):
                    ls = (hb % 2) if lsfix is None else lsfix
                    ln_ = "Lt%d" % ls
                    for j in range(4):
                        h = hb * 4 + j
                        first = (not with_skip) and (h % 8 == 0)
                        mm(P[pyb][:, (h % 8) * 64:(h % 8 + 1) * 64], Lt[ls][:, j, :], XX[:, 0, h * 64:(h + 1) * 64],
                           first, [ln_, "XX"], pyb)

                if lsfix is None:
                    hb_a(2 * g)
                    yield
                    hb_a(2 * g + 1)
                    yield
                    hb_b(2 * g)
                    yield
                    hb_b(2 * g + 1)
                    yield
                else:
                    for hb in (2 * g, 2 * g + 1):
                        hb_a(hb)
                        yield
                        hb_b(hb)
                        yield
                bn = nextbank()
                mm(P[bn][:], xbcT[:, 10 + g, :], Sbf[:, g * 512:(g + 1) * 512], True, ["xbcT", "Sbf"], bn)
                tt("dve", y1[:, g * 512:(g + 1) * 512].rearrange("p (h d) -> p h d", h=8),
                   P[bn][:].rearrange("p (h d) -> p h d", h=8), bc3(smc(ss, 64 + g * 8, 8), 64),
                   ALU.mult, [bn, n_("ecs")], ["y1"])
                tt("dve", y1[:, g * 512:(g + 1) * 512], y1[:, g * 512:(g + 1) * 512], P[pyb][:], ALU.add,
                   ["y1", pyb], ["y1"])
                yield
            if bwd:
                gg_ = [grp(0, "PY", 0), grp(1, "OA", 1)]
                while gg_:
                    for q_ in list(gg_):
                        try:
                            next(q_)
                        except StopIteration:
                            gg_.remove(q_)
                    yield
            else:
                for g in range(2):
                    yield from grp(g, "PY", None)
        tt("pool", Sst[:].rearrange("p (h d) -> p h d", h=16), Sst[:].rearrange("p (h d) -> p h d", h=16),
           bc3(smc(ss, 96), 64), ALU.mult, ["Sst", n_("etot")], ["Sst"])
        for g in range(2):
            bn = nextbank()
            mm(P[bn][:], Bt[:, g, :], XX[:, 1, g * 512:(g + 1) * 512], True, ["Bt", "XX"], bn)
            tt("dve", Sst[:, g * 512:(g + 1) * 512], Sst[:, g * 512:(g + 1) * 512], P[bn][:], ALU.add,
               ["Sst", bn], ["Sst"])
        cp("dve", Sbf[:], Sst[:], ["Sst"], ["Sbf"])
        yield

    def attention(i, has_prev):
        blocks = ([0] if has_prev else []) + [1, 2]
        obanks = ["OA", "OB", "OC"]

        def oslot(h):
            return obanks[h // 7], (h % 7) * 65

        firstw = {b_: True for b_ in obanks}
        bx = ("PA", "PB")
        rounds = [(gp, o) for gp in range(2) for o in blocks]

        def step_a(r):
            gp, o = rounds[r]
            ks = (i + o - 1) % 3
            for par in range(2):
                bi0 = gp * 8 + par * 4
                mm(P[bx[par]][:], cb[:, B_ID:B_ID + 128],
                   biasT[:, o, bi0:bi0 + 4, :].rearrange("p a b -> p (a b)"), True, ["cb", "biasT"], bx[par])
            for k4 in range(4):
                for par in range(2):
                    h = 8 * gp + 2 * k4 + par
                    g, j = h // 4, h // 2
                    mm(P[bx[par]][:, k4 * 128:(k4 + 1) * 128], KT[ks][par * 64:(par + 1) * 64, g, :],
                       QT[par * 64:(par + 1) * 64, j, :], False, ["KT%d" % ks, "QT"], bx[par])
            act(pT[r % 2][:], PAB[:], AF.Exp, ["PA", "PB"], ["pT%d" % (r % 2)])

        def step_b(r):
            gp, o = rounds[r]
            ks = (i + o - 1) % 3
            pn = "pT%d" % (r % 2)
            for par in range(2):
                for k4 in range(4):
                    h = 8 * gp + 2 * k4 + par
                    ob, oc = oslot(h)
                    c0 = par * 512 + k4 * 128
                    mm(P[ob][:, oc:oc + 65], pT[r % 2][:, c0:c0 + 128], Va[ks][:, h // 4, :],
                       firstw[ob], [pn, "Va%d" % ks], ob)
                    firstw[ob] = False

        step_a(0)
        yield
        for r in range(len(rounds)):
            if r + 1 < len(rounds):
                step_a(r + 1)
                yield
            step_b(r)
            yield

    def attn_epi():
        obanks = ["OA", "OB", "OC"]
        for bi, ob in enumerate(obanks):
            nh = 7 if bi < 2 else 2
            v = P[ob][:, 0:nh * 65].rearrange("p (h d) -> p h d", h=nh)
            tt("dve", sm[:, 352 + bi * 7:352 + bi * 7 + nh].unsqueeze(2), v[:, :, 64:65],
               esink[:, bi * 7:bi * 7 + nh].unsqueeze(2), ALU.add, [ob, "esink"], ["sm_den"])
        S.op("dve", lambda g_: g_.reciprocal(sm[:, 368:384], sm[:, 352:368]), reads=[buf("sm_den")],
             writes=[buf("sm_rden")])
        for bi, ob in enumerate(obanks):
            nh = 7 if bi < 2 else 2
            v = P[ob][:, 0:nh * 65].rearrange("p (h d) -> p h d", h=nh)
            tt("dve", o1[:, bi * 448:bi * 448 + nh * 64].rearrange("p (h d) -> p h d", h=nh), v[:, :, 0:64],
               bc3(sm[:, 368 + bi * 7:368 + bi * 7 + nh], 64), ALU.mult, [ob, "sm_rden"], ["o1"])
        yield
        tt("dve", o1[:], o1[:], gas[:], ALU.mult, ["o1", "gas"], ["o1"])
        sumsq(o1[:], D, 8, ["o1"], "sm_ssa")
        yield
        rstd_of(sm[:, 8:9], D, sm[:, 9:10], ["sm_ssa"], ["sm_rsa"])
        yield
        act(catbf[:, D:2 * D], o1[:], AF.Copy, ["o1", "sm_rsa"], ["catbf"], scale=sm[:, 9:10])
        yield

    def proj_zg(slot):
        hs = HN[slot]
        for (col0, dst, dname) in ((TM_Z, zs, "zs"), (TM_GA, gas, "gas")):
            for hf in range(2):
                bn = nextbank()
                for kc in range(8):
                    mm(P[bn][:], hT[slot][:, kc, :], w_sb[:, kc, col0 + hf * 512:col0 + (hf + 1) * 512], kc == 0,
                       ["w_fwd", hs], bn)
                act(dst[:, hf * 512:(hf + 1) * 512], P[bn][:], AF.Silu, [bn], [dname])
                yield

    def proj_q(slot):
        hs = HN[slot]
        for r in range(2):
            bn = nextbank()
            for c in range(4):
                ch = r * 4 + c
                for kc in range(8):
                    mm(P[bn][:, c * 128:(c + 1) * 128], w_sb[:, kc, FM + 1536 + ch * 128:FM + 1536 + (ch + 1) * 128],
                       hT[slot][:, kc, :], (c == 0 and kc == 0), ["w_fwd", hs], bn)
            cp("dve", QT[:, r * 4:(r + 1) * 4, :].rearrange("p a b -> p (a b)"), P[bn][:], [bn], ["QT"])
            yield

    def ssd_fwd_main(i):
        dma(t2[:], yb_d[i * T:(i + 1) * T, :], ["ybwd_dram"], ["t2"])
        yield from ssd(i % 2, False, False, True)

    def ssd_epi(i):
        tt("dve", y1[:], y1[:], t2[:], ALU.add, ["y1", "t2"], ["y1"])
        tt("dve", y1[:], y1[:], zs[:], ALU.mult, ["y1", "zs"], ["y1"])
        dma(t2[:], x_d[i * T:(i + 1) * T, :], [], ["t2"])
        yield
        for g in range(2):
            sumsq(y1[:, g * 512:(g + 1) * 512], 512, 4 + g, ["y1"], "sm_ssy%d" % g)
        rstd_of(sm[:, 4:6], 512, sm[:, 6:8], ["sm_ssy0", "sm_ssy1"], ["sm_rsy"])
        yield
        for g in range(2):
            act(catbf[:, g * 512:(g + 1) * 512], y1[:, g * 512:(g + 1) * 512], AF.Copy, ["y1", "sm_rsy"], ["catbf"],
                scale=sm[:, 6 + g:7 + g])
        yield

    def after(flags, keys, g):
        while not all(flags.get(k_) for k_ in keys):
            yield
        yield from g

    def par(*gs):
        gens = [g for g in gs if g is not None]
        while gens:
            for g in list(gens):
                try:
                    next(g)
                except StopIteration:
                    gens.remove(g)
            yield

    def epi_chain(i, extra=None, flags=None):
        yield from par(ssd_epi(i), attn_epi())
        for _ in range(int(os.environ.get("KWARM_TAIL", "12"))):
            mm(P["OC"][:, 256:512], cb[:, B_ID:B_ID + 128], cb[:, B_MNF:B_MNF + 256], False,
               ["cb", "sm_rsa", "sm_rsy"], "OC")
        yield from par(tail_H(i, flags), extra)

    def tail_H(i, flags=None):
        catT = XX[:].rearrange("p a (c t) -> p (a c) t", t=T)
        for r in range(2):
            for c in range(8):
                tr(PT[:, c * 128:(c + 1) * 128], catbf[:, (r * 8 + c) * 128:(r * 8 + c + 1) * 128], ["catbf"], "PT")
            cp("act" if r else "dve", catT[:, r * 8:(r + 1) * 8, :].rearrange("p a b -> p (a b)"), PT[:], ["PT"],
               ["XX"])
            yield
        for hf in range(2):
            bn = nextbank()
            for kc in range(16):
                mm(P[bn][:], catT[:, kc, :], wo_sb[:, kc, hf * 512:(hf + 1) * 512], kc == 0, ["XX", "wo_sb"], bn)
            tt("dve", t2[:, hf * 512:(hf + 1) * 512], t2[:, hf * 512:(hf + 1) * 512], P[bn][:], ALU.add,
               ["t2", bn], ["t2"])
            yield
        if flags is not None:
            flags["op"] = True
        sumsq(t2[:], D, 10, ["t2"], "sm_sso")
        rstd_of(sm[:, 10:11], D, sm[:, 11:12], ["sm_sso"], ["sm_rso"])
        S.op("dve", lambda g_: g_.scalar_tensor_tensor(out=y1[:], in0=t2[:], scalar=sm[:, 11:12],
                                                       in1=cf[:, C_FG:C_FG + D], op0=ALU.mult, op1=ALU.mult),
             reads=[buf("t2"), buf("sm_rso"), buf("cf")], writes=[buf("y1")])
        dma(out_d[i * T:(i + 1) * T, :], y1[:], ["y1"], ["out_dram"])
        yield

    def ssd_bwd_chain(i, states_only):
        yield from ssd(i % 2, True, states_only, False)
        if not states_only:
            dma(yb_d[i * T:(i + 1) * T, :], y1[:], ["y1"], ["ybwd_dram"])
        yield

    _run([front(NT - 1, (NT - 1) % 3, False)])
    zero_halo((NT - 1) % 3, left=False)
    _run([front(NT - 2, (NT - 2) % 3, False), dt_and_decay((NT - 1) % 3, 16, True, ss=(NT - 1) % 2)])
    halo((NT - 1) % 3, (NT - 2) % 3, True)
    _run([front(NT - 3, (NT - 3) % 3, False)])
    halo((NT - 2) % 3, (NT - 3) % 3, True)
    bg = bg_setup()
    for i in reversed(range(NT)):
        slot = i % 3
        so = (i >= NTH)
        conv_silu(slot, 10 if so else 12)
        if not so:
            dma(xbc_d[i], xbcT[:].rearrange("p a b -> p (a b)"), ["xbcT"], ["xbc_dram"])
        gens = []
        if i - 3 >= 0:
            gens.append(front(i - 3, slot, False))
        if i - 1 >= 0:
            gens.append(dt_and_decay((i - 1) % 3, 16, True, ss=(i - 1) % 2))
        if so:
            gens.append(ssd_bwd_chain(i, so))
        else:
            gens.insert(0, ssd_bwd_chain(i, so))
        rv = os.environ.get("KWARM_REV", "0")
        S.warm_fn = warm if (NWARM and (rv == "1" or (rv == "own" and not so) or (rv == "far" and so))) else None
        _run(gens, bg if so else None)
        S.warm_fn = None
        if i == NTH:
            _run([bg])
        if i - 3 >= 0:
            halo((i - 2) % 3, slot, True)
            if i - 3 == 0:
                zero_halo(slot, left=True)

    S.op("pool", lambda g: g.memset(Sst[:], 0.0), writes=[buf("Sst")])
    S.op("pool", lambda g: g.memset(Sbf[:], 0.0), writes=[buf("Sbf")])
    def load_xbc(i):
        dma(xbcT[:].rearrange("p a b -> p (a b)"), xbc_d[i], ["xbc_dram"], ["xbcT"])

    def front_kvonly(i, slot):
        yield from front(i, slot, True, head_only=True)
        yield from front_kv(i, slot)

    _run([front_kvonly(0, 0)])
    load_xbc(0)
    _run([front_kvonly(1, 1), dt_and_decay(0, 0, False), proj_q(0)])
    _run([proj_zg(0)])
    def one_step(g):
        next(g)
        yield

    g_ssd = ssd_fwd_main(0)
    for i in range(NTH):
        slot = i % 2
        gens = [g_ssd, attention(i, i > 0)]
        if i + 1 < NTH:
            gens.append(dt_and_decay(1 - slot, 0, False))
        if i + 2 <= NTH:
            gens.append(front(i + 2, slot, True, head_only=True))
        S.warm_fn = warm if NWARM else None
        _run(gens)
        S.warm_fn = None
        if i + 1 < NTH:
            load_xbc(i + 1)
        fl = {}
        gens = [epi_chain(i, proj_zg(1 - slot) if i + 1 < NTH else None, fl)]
        if i + 2 <= NTH:
            gens.append(front_kv(i + 2, slot))
        if i + 1 < NTH:
            gens.append(proj_q(1 - slot))
            g_ssd = ssd_fwd_main(i + 1)
            gens.append(after(fl, ["op"], one_step(g_ssd)))
        _run(gens)
    S.finish()
    return nc


def _t5_bucket_table():
    import jax
    import jax.numpy as jnp
    with jax.default_device(jax.devices("cpu")[0]):
        return _t5_bucket_table_impl(jnp)


def _t5_bucket_table_impl(jnp):
    rel = jnp.arange(-128, 129)
    half = 16
    ret = (rel > 0).astype(jnp.int32) * half
    n = jnp.abs(rel)
    is_small = n < 8
    nf = jnp.maximum(n, 1).astype(jnp.float32)
    large = 8 + (jnp.log(nf / 8) / math.log(128 / 8) * (half - 8)).astype(jnp.int32)
    large = jnp.minimum(large, half - 1)
    return np.asarray(ret + jnp.where(is_small, n, large))


def _consts_bf():
    idx = np.arange(128)
    c = np.zeros((128, NCB), np.float32)
    c[:, B_ID:B_ID + 128] = np.eye(128)
    c[:, B_TRF:B_TRF + 128] = (idx[:, None] <= idx[None, :])
    c[:, B_TRB:B_TRB + 128] = (idx[:, None] >= idx[None, :])
    c[:, B_ONE:B_ONE + 128] = 1.0
    c[:, B_MNF:B_MNF + 512] = np.tile(np.where(idx[:, None] > idx[None, :], NEG, 0.0), (1, 4))
    c[:, B_MNB:B_MNB + 512] = np.tile(np.where(idx[:, None] < idx[None, :], NEG, 0.0), (1, 4))
    return c.astype(ml_dtypes.bfloat16)


def _prep_core(inp, b, half, NTH, bucket):
    f32 = np.float32
    x = inp["x"][b]
    if half:
        x = x[::-1]
    x = np.ascontiguousarray(x, dtype=f32)
    w_in = np.asarray(inp["w_in"], f32)
    z, xbc, dtc = w_in[:, 0:1024], w_in[:, 1024:2560], w_in[:, 2560:2592]
    q, k, v, ga = w_in[:, 2592:3616], w_in[:, 3616:3872], w_in[:, 3872:4128], w_in[:, 4128:5152]
    d0, d1 = dtc[:, 0:16], dtc[:, 16:32]
    if half:
        d0, d1 = d1, d0
    kd = np.concatenate([np.concatenate([k[:, i * 64:(i + 1) * 64]] * 2, axis=1) for i in range(4)], axis=1)
    w = np.ascontiguousarray(np.concatenate([z, ga, v, d0, d1, xbc, q, kd], axis=1))
    assert w.shape[1] == NCOLS
    idx = np.arange(128)
    cfa = np.zeros((128, NCF), f32)
    cfa[:, C_M01F:C_M01F + 128] = (idx[:, None] <= idx[None, :])
    cfa[:, C_M01B:C_M01B + 128] = (idx[:, None] >= idx[None, :])
    order = [1, 0] if half else [0, 1]
    cfa[:, C_DTB:C_DTB + 32] = np.asarray(inp["dt_bias"], f32)[order].reshape(1, 32)
    cfa[:, C_ALOG:C_ALOG + 32] = np.asarray(inp["a_log"], f32)[order].reshape(1, 32)
    cfa[:, C_SINK:C_SINK + 16] = np.asarray(inp["sink"], f32).reshape(1, 16)
    cfa[:, C_FG:C_FG + D] = np.asarray(inp["final_norm_g"], f32).reshape(1, D)
    cfa[:, C_GIN:C_GIN + 8] = np.asarray(inp["norm_in_g"], f32).reshape(8, 128).T
    cfa[:, C_GSSD:C_GSSD + 8] = np.asarray(inp["ssd_norm_g"], f32).reshape(8, 128).T
    cfa[:, C_GATT:C_GATT + 8] = np.asarray(inp["attn_norm_g"], f32).reshape(8, 128).T
    cfa[:, C_CB:C_CB + 12] = np.asarray(inp["conv_b"], f32).reshape(12, 128).T
    cw = np.asarray(inp["conv_w"], f32)
    if half:
        cw = cw[::-1]
    cfa[:, C_CW:C_CW + 60] = cw.reshape(5, 12, 128).transpose(2, 1, 0).reshape(128, 60)
    cfa[:, C_DSK:C_DSK + 8] = np.repeat(np.asarray(inp["d_skip"], f32), 64).reshape(8, 128).T
    cfa[:, C_NH] = -0.5
    rb = np.asarray(inp["rel_bias"], f32)
    tq = idx[:, None] - idx[None, :]
    bt = np.full((128, 3, 16, 128), NEG, f32)
    for o in range(3):
        rl = (o - 1) * 128 + tq
        valid = np.abs(rl) <= 128
        rg = -rl if half else rl
        bidx = bucket[np.clip(rg, -128, 128) + 128]
        vals = rb[bidx]
        hperm = [8 * gp + 2 * k4 + par for gp in range(2) for par in range(2) for k4 in range(4)]
        bt[:, o] = np.where(valid[:, None, :], vals.transpose(0, 2, 1)[:, hperm, :], NEG)
    return {
        "x": x, "w_in": w, "w_out": np.ascontiguousarray(np.asarray(inp["w_out"], f32)[0]),
        "cst_f32": cfa, "cst_bf": _consts_bf(),
        "biasT": np.ascontiguousarray(bt.reshape(128, -1)).astype(ml_dtypes.bfloat16),
    }


_NC_CACHE = {}


def kernel(**inputs):
    NTH = NTH_FULL
    x = np.asarray(inputs["x"])
    Bsz, Sq, _ = x.shape
    assert Sq == 2 * NTH * T and Bsz == 4
    bucket = _t5_bucket_table()
    in_maps = [_prep_core(inputs, c // 2, c % 2, NTH, bucket) for c in range(8)]
    if NTH not in _NC_CACHE:
        _NC_CACHE[NTH] = _build(NTH)
    nc = _NC_CACHE[NTH]
    res = run_bass_kernel_spmd(nc, in_maps, core_ids=list(range(8)))
    out = np.empty((Bsz, Sq, D), np.float32)
    h = NTH * T
    for c in range(8):
        o = np.asarray(res.results[c]["out"], np.float32)
        if c % 2 == 0:
            out[c // 2, 0:h] = o
        else:
            out[c // 2, h:] = o[::-1]
    return out
```

```python
import math
import os
import numpy as np
import ml_dtypes
import concourse.bass as bass
import concourse.mybir as mybir
from concourse.bass_utils import run_bass_kernel_spmd

F32 = mybir.dt.float32
BF16 = mybir.dt.bfloat16
AF = mybir.ActivationFunctionType
ALU = mybir.AluOpType

T = 128
D = 1024
NTH_FULL = 32
TM_Z, TM_GA, TM_V, TM_DT, FM = 0, 1024, 2048, 2304, 2336
NCOLS = 5408
EPS = 1e-6
NEG = -30000.0

C_M01F, C_M01B, C_DTB, C_ALOG, C_SINK, C_FG = 0, 128, 256, 288, 320, 336
C_GIN, C_GSSD, C_GATT, C_CB, C_CW, C_DSK, C_NH, NCF = 1360, 1368, 1376, 1384, 1396, 1456, 1464, 1472
B_ID, B_TRF, B_TRB, B_ONE, B_MNF, B_MNB, NCB = 0, 128, 256, 384, 512, 1024, 1536


class _Buf:
    __slots__ = ("name", "last_w", "readers", "excl")

    def __init__(self, name, excl=False):
        self.name = name
        self.last_w = None
        self.readers = []
        self.excl = excl


class _Sched:
    NDMA = 8

    def __init__(self, nc):
        self.nc = nc
        self.eng = {"pe": nc.tensor, "act": nc.scalar, "dve": nc.vector, "pool": nc.gpsimd, "sp": nc.sync}
        names = list(self.eng) + ["dma%d" % j for j in range(self.NDMA)]
        self.sem = {n: nc.alloc_semaphore("s_" + n) for n in names}
        self.cnt = {n: 0 for n in names}
        self.inc = {n: (16 if n.startswith("dma") else 1) for n in names}
        self.seen = {e: {n: 0 for n in names} for e in self.eng}
        self.rr = 0

    def _deps(self, e, reads, writes):
        deps = {}

        def need(dep, raw):
            if dep is None:
                return
            f, idx = dep
            if f == e and (e == "pe" or not raw):
                return
            if deps.get(f, 0) < idx:
                deps[f] = idx

        for b in reads:
            need(b.last_w, True)
            if b.excl:
                for r in b.readers:
                    need(r, False)
        for b in writes:
            need(b.last_w, False)
            for r in b.readers:
                need(r, False)
        return deps

    warm_fn = None
    _in_warm = False

    pe_pending = None

    def _finalize_pe(self):
        if self.pe_pending is not None:
            self.pe_pending.then_inc(self.sem["pe"], 1)
            self.cnt["pe"] += 1
            self.pe_pending = None

    def _wait(self, e, deps):
        eng = self.eng[e]
        if e != "pe" and deps.get("pe", 0) > self.cnt["pe"]:
            self._finalize_pe()
        if e == "pe" and self.warm_fn is not None and not self._in_warm:
            if any(self.seen[e][f] < idx for f, idx in deps.items()):
                self._in_warm = True
                self.warm_fn()
                self._in_warm = False
        for f, idx in deps.items():
            if self.seen[e][f] >= idx:
                continue
            eng.wait_ge(self.sem[f], idx * self.inc[f])
            self.seen[e][f] = idx

    def _mark(self, me, reads, writes):
        for b in writes:
            b.last_w = me
            b.readers = []
        for b in reads:
            if b not in writes:
                b.readers.append(me)

    def op(self, e, fn, reads=(), writes=()):
        self._wait(e, self._deps(e, reads, writes))
        inst = fn(self.eng[e])
        if e == "pe":
            self.pe_pending = inst
            self._mark((e, self.cnt[e] + 1), reads, writes)
            return
        self.cnt[e] += 1
        inst.then_inc(self.sem[e], 1)
        self._mark((e, self.cnt[e]), reads, writes)

    def dma(self, out, in_, reads=(), writes=()):
        d = "dma%d" % self.rr
        self.rr = (self.rr + 1) % self.NDMA
        deps = self._deps("sp", reads, writes)
        if self.cnt[d] > 0:
            deps[d] = max(deps.get(d, 0), self.cnt[d])
        self._wait("sp", deps)
        inst = self.nc.sync.dma_start(out=out, in_=in_)
        self.cnt[d] += 1
        inst.then_inc(self.sem[d], 16)
        self._mark((d, self.cnt[d]), reads, writes)

    def finish(self):
        self._finalize_pe()
        deps = {n: c for n, c in self.cnt.items() if c > 0 and n != "sp"}
        self._wait("sp", deps)


def _run(gens, bg=None):
    gens = [g for g in gens if g is not None]
    while gens:
        for g in list(gens):
            try:
                next(g)
            except StopIteration:
                gens.remove(g)
        if bg is not None:
            try:
                next(bg)
            except StopIteration:
                bg = None


def _build(NTH):
    NT = 2 * NTH
    nc = bass.Bass("TRN2", target_bir_lowering=False, dynamic_dma_scratch_size=512)
    x_d = nc.dram_tensor("x", [NT * T, D], F32, kind="ExternalInput").ap()
    w_d = nc.dram_tensor("w_in", [D, NCOLS], F32, kind="ExternalInput").ap()
    wo_d = nc.dram_tensor("w_out", [2 * D, D], F32, kind="ExternalInput").ap()
    cf_d = nc.dram_tensor("cst_f32", [128, NCF], F32, kind="ExternalInput").ap()
    cb_d = nc.dram_tensor("cst_bf", [128, NCB], BF16, kind="ExternalInput").ap()
    bt_d = nc.dram_tensor("biasT", [128, 3 * 16 * 128], BF16, kind="ExternalInput").ap()
    out_d = nc.dram_tensor("out", [NTH * T, D], F32, kind="ExternalOutput").ap()
    yb_d = nc.dram_tensor("ybwd", [NTH * T, D], F32, kind="Internal").ap()
    xbc_d = nc.dram_tensor("xbcs", [NTH, 128, 1536], BF16, kind="Internal").ap()

    def sb(name, shape, dt):
        return nc.alloc_sbuf_tensor(name, shape, dt)

    w_sb = sb("w_sb", [128, 8, NCOLS], BF16)
    wo_sb = sb("wo_sb", [128, 16, D], BF16)
    biasT = sb("biasT_sb", [128, 3, 16, 128], BF16)
    dconv = sb("dconv", [128, 60, 128], BF16)
    dskd = sb("dskd", [128, 8, 128], BF16)
    cf = sb("cf", [128, NCF], F32)
    cb = sb("cb", [128, NCB], BF16)
    xf = sb("xf", [128, D], F32)
    Sst = sb("Sst", [128, D], F32)
    y1 = sb("y1", [128, D], F32)
    t2 = sb("t2", [128, D], F32)
    o1 = sb("o1", [128, D], F32)
    hbf = sb("hbf", [128, D], BF16)
    jk = sb("jk", [128, D], BF16)
    hT = [sb("hT%d" % i, [128, 8, T], BF16) for i in range(2)]
    xpad = [sb("xpad%d" % i, [128, 12, 132], BF16) for i in range(2)]
    xbcT = sb("xbcT", [128, 12, T], BF16)
    QT = sb("QT", [128, 8, T], BF16)
    KT = [sb("KT%d" % i, [128, 4, T], BF16) for i in range(3)]
    Va = [sb("Va%d" % i, [128, 4, 65], BF16) for i in range(3)]
    zs = sb("zs", [128, D], BF16)
    gas = sb("gas", [128, D], BF16)
    XX = sb("XX", [128, 2, D], BF16)
    Bt = sb("Bt", [128, 2, T], BF16)
    Lt = [sb("Lt%d" % i, [128, 4, T], BF16) for i in range(2)]
    CBm = sb("CBm", [128, 2, T], F32)
    Sbf = sb("Sbf", [128, D], BF16)
    pT = [sb("pT%d" % i, [128, 512], BF16) for i in range(4)]
    catbf = sb("catbf", [128, 2 * D], BF16)
    sm = sb("sm", [128, 384], F32)
    smb = sb("smb", [128, 64], BF16)
    A_bc = sb("A_bc", [128, 32], F32)
    esink = sb("esink", [128, 16], F32)

    HN = ["hT0", "hT1", "QT"]
    XN = ["xpad0", "xpad1", "catbf"]
    hT.append(QT)
    xpad.append(catbf[:, 0:12 * 132].rearrange("p (a b) -> p a b", a=12))

    P = {n: nc.alloc_psum_tensor(n, [128, 512], F32) for n in ["PA", "PB", "PS", "PY", "OA", "OB", "OC"]}
    PT = nc.alloc_psum_tensor("PT", [128, 1024], BF16)

    S = _Sched(nc)
    B = {}

    def buf(name, excl=False):
        if name not in B:
            B[name] = _Buf(name, excl)
        return B[name]

    for n in list(P) + ["PT"]:
        buf(n, True)

    def bk(n):
        return B[n]

    def mm(outap, lhsT, rhs, start, reads, bank):
        S.op("pe", lambda e: e.matmul(outap, lhsT, rhs, start=start, stop=True, skip_group_check=True),
             reads=[buf(r) for r in reads], writes=[bk(bank)])

    def tr(outap, inap, reads, bank):
        S.op("pe", lambda e: e.transpose(outap, inap, cb[:, B_ID:B_ID + 128]),
             reads=[buf(r) for r in reads] + [buf("cb")], writes=[bk(bank)])

    def act(outap, inap, func, reads, writes, **kw):
        S.op("act", lambda e: e.activation(outap, inap, func, **kw),
             reads=[buf(r) for r in reads], writes=[buf(w) for w in writes])

    def tt(e, outap, a, b_, op, reads, writes):
        S.op(e, lambda g: g.tensor_tensor(outap, a, b_, op),
             reads=[buf(r) for r in reads], writes=[buf(w) for w in writes])

    def ts(e, outap, a, s1, s2, op0, op1, reads, writes):
        S.op(e, lambda g: g.tensor_scalar(outap, a, s1, s2, op0, op1),
             reads=[buf(r) for r in reads], writes=[buf(w) for w in writes])

    def cp(e, outap, inap, reads, writes):
        if e == "act":
            S.op(e, lambda g: g.activation(outap, inap, AF.Copy),
                 reads=[buf(r) for r in reads], writes=[buf(w) for w in writes])
        else:
            S.op(e, lambda g: g.tensor_copy(outap, inap),
                 reads=[buf(r) for r in reads], writes=[buf(w) for w in writes])

    def scale_cast(e, outap, inap, col, extra, reads, writes):
        if e == "act":
            if extra != 1.0:
                raise ValueError
            act(outap, inap, AF.Copy, reads + ["cf"], writes, scale=cf[:, col:col + 1])
        elif extra != 1.0:
            ts(e, outap, inap, cf[:, col:col + 1], extra, ALU.mult, ALU.mult, reads + ["cf"], writes)
        else:
            ts(e, outap, inap, cf[:, col:col + 1], None, ALU.mult, ALU.bypass, reads + ["cf"], writes)

    def dma(outap, inap, reads, writes):
        S.dma(outap, inap, reads=[buf(r) for r in reads], writes=[buf(w) for w in writes])

    def bc3(ap2, n):
        return ap2.unsqueeze(2).to_broadcast([128, ap2.shape[1], n])

    NWARM = int(os.environ.get("KWARM", "1"))

    def warm():
        for _ in range(NWARM):
            mm(P["OC"][:, 256:512], cb[:, B_ID:B_ID + 128], cb[:, B_MNF:B_MNF + 256], False, ["cb"], "OC")

    def wname(c0):
        return "w_rev" if 2304 <= c0 < 3872 else "w_fwd"

    dma(cf[:], cf_d, [], ["cf"])
    dma(cb[:], cb_d, [], ["cb"])
    q0, q1 = FM + 1536, FM + 2560
    kq = [0]

    def load_w(kc, c0, w, st, sn):
        dma(st[:, 0:w], w_d[kc * 128:(kc + 1) * 128, c0:c0 + w], [], [sn])
        isq = (q0 <= c0 < q1)
        assert isq == (q0 <= c0 + w - 1 < q1)
        kq[0] += 1
        e = "dve" if (isq or kq[0] % 2) else "act"
        scale_cast(e, w_sb[:, kc, c0:c0 + w], st[:, 0:w], C_GIN + kc, 0.125 if isq else 1.0, [sn], [wname(c0)])

    stg = [(xf, "xf"), (y1, "y1"), (t2, "t2"), (o1, "o1")]
    k = 0
    for kc in range(8):
        for (c0, w) in [(2304, 784), (3088, 784)]:
            st, sn = stg[k % 4]
            k += 1
            load_w(kc, c0, w, st, sn)
    for c in range(12):
        for kk in range(5):
            j = c * 5 + kk
            scale_cast("dve" if j % 2 else "act", dconv[:, j, :], cb[:, B_ID:B_ID + 128], C_CW + j, 1.0,
                       ["cb"], ["dconv"])
    for c in range(8):
        scale_cast("dve", dskd[:, c, :], cb[:, B_ID:B_ID + 128], C_DSK + c, 1.0, ["cb"], ["dskd"])
    act(A_bc[:], cf[:, C_ALOG:C_ALOG + 32], AF.Exp, ["cf"], ["A_bc"])
    ts("dve", A_bc[:], A_bc[:], -1.0, None, ALU.mult, ALU.bypass, ["A_bc"], ["A_bc"])
    act(esink[:], cf[:, C_SINK:C_SINK + 16], AF.Exp, ["cf"], ["esink"])
    for i in range(3):
        S.op("pool", lambda g, i=i: g.memset(Va[i][:, :, 64:65], 1.0), writes=[buf("Va%d" % i)])
    for i in range(2):
        S.op("pool", lambda g, i=i: g.memset(xpad[i][:], 0.0), writes=[buf("xpad%d" % i)])
    S.op("pool", lambda g: g.memset(Sst[:], 0.0), writes=[buf("Sst")])
    S.op("pool", lambda g: g.memset(Sbf[:], 0.0), writes=[buf("Sbf")])
    S.op("pool", lambda g: g.memset(catbf[:], 0.0), writes=[buf("catbf")])
    S.op("dve", lambda g: g.memset(P["OC"][:], 0.0), writes=[bk("OC")])

    def bg_setup():
        st3 = [(y1, "y1"), (t2, "t2"), (o1, "o1")]
        kk_ = 0
        dma(biasT[:].rearrange("p a b c -> p (a b c)"), bt_d, [], ["biasT"])
        yield
        cols = [(0, 1024), (1024, 1024), (2048, 256), (3872, 1024), (4896, 512)]
        for kc in range(8):
            for (c0, w) in cols:
                st, sn = st3[kk_ % 3]
                kk_ += 1
                load_w(kc, c0, w, st, sn)
                yield
        for kc in range(16):
            st, sn = st3[kk_ % 3]
            kk_ += 1
            dma(st[:], wo_d[kc * 128:(kc + 1) * 128, :], [], [sn])
            gcol = (C_GSSD + kc) if kc < 8 else (C_GATT + kc - 8)
            scale_cast("dve" if kk_ % 2 else "act", wo_sb[:, kc, :], st[:], gcol, 1.0, [sn], ["wo_sb"])
            yield

    rot = [0]

    def nextbank():
        rot[0] ^= 1
        return "PA" if rot[0] else "PB"

    def rstd_of(ss_ap, n, out_ap, rd, wr, eps=EPS):
        ts("dve", out_ap, ss_ap, 1.0 / n, eps, ALU.mult, ALU.add, rd, wr)
        k_ = out_ap.shape[1]
        tt("pool", out_ap, out_ap, cf[:, C_NH:C_NH + 1].to_broadcast([128, k_]), ALU.pow, wr + ["cf"], wr)

    def sumsq(src_ap, n_el, out_col, rd, wr):
        act(jk[:, 0:n_el], src_ap, AF.Square, rd + ["jk"], ["jk", wr], accum_out=sm[:, out_col:out_col + 1])

    def front(i, slot, want_kv, head_only=False):
        hs = HN[slot]
        dma(xf[:], x_d[i * T:(i + 1) * T, :], [], ["xf"])
        sumsq(xf[:], D, 0, ["xf"], "sm_ss")
        rstd_of(sm[:, 0:1], D, sm[:, 1:2], ["sm_ss"], ["sm_rs"])
        act(hbf[:], xf[:], AF.Copy, ["xf", "sm_rs"], ["hbf"], scale=sm[:, 1:2])
        yield
        if head_only == "a":
            return
        yield from front_b(i, slot, want_kv, head_only)

    def front_b(i, slot, want_kv, head_only=False):
        hs = HN[slot]
        for kc in range(8):
            tr(PT[:, kc * 128:(kc + 1) * 128], hbf[:, kc * 128:(kc + 1) * 128], ["hbf"], "PT")
        cp("dve", hT[slot][:].rearrange("p a b -> p (a b)"), PT[:], ["PT"], [hs])
        yield
        if head_only:
            return
        yield from front_tail(i, slot, want_kv)

    def front_tail(i, slot, want_kv):
        hs = HN[slot]
        xs_ = XN[slot]
        nch = 10 if i > NTH else 12
        for r in range(3):
            bn = nextbank()
            ncr = min(4, nch - r * 4)
            for c in range(ncr):
                ch = r * 4 + c
                for kc in range(8):
                    mm(P[bn][:, c * 128:(c + 1) * 128], w_sb[:, kc, FM + ch * 128:FM + (ch + 1) * 128],
                       hT[slot][:, kc, :], (c == 0 and kc == 0), ["w_rev", hs], bn)
            cp("act" if r == 1 else "dve", xpad[slot][:, r * 4:r * 4 + ncr, 2:130],
               P[bn][:, 0:ncr * 128].rearrange("p (a b) -> p a b", a=ncr), [bn], [xs_])
            yield
        if want_kv:
            yield from front_kv(i, slot)

    def front_kv(i, slot):
        hs = HN[slot]
        if True:
            ks = i % 3
            bn = nextbank()
            for c in range(4):
                for kc in range(8):
                    mm(P[bn][:, c * 128:(c + 1) * 128], w_sb[:, kc, FM + 2560 + c * 128:FM + 2560 + (c + 1) * 128],
                       hT[slot][:, kc, :], (c == 0 and kc == 0), ["w_fwd", hs], bn)
            cp("act", KT[ks][:].rearrange("p a b -> p (a b)"), P[bn][:], [bn], ["KT%d" % ks])
            yield
            bn = nextbank()
            for kc in range(8):
                mm(P[bn][:, 0:256], hT[slot][:, kc, :], w_sb[:, kc, TM_V:TM_V + 256], kc == 0, ["w_fwd", hs], bn)
            cp("dve", Va[ks][:, :, 0:64], P[bn][:, 0:256].rearrange("p (a b) -> p a b", a=4), [bn], ["Va%d" % ks])
            yield

    def halo(cur, nxt, nxt_precedes):
        a, b_ = XN[cur], XN[nxt]
        if nxt_precedes:
            cp("pool", xpad[cur][:, :, 0:2], xpad[nxt][:, :, 128:130], [b_], [a])
            cp("pool", xpad[nxt][:, :, 130:132], xpad[cur][:, :, 2:4], [a], [b_])
        else:
            cp("pool", xpad[cur][:, :, 130:132], xpad[nxt][:, :, 2:4], [b_], [a])
            cp("pool", xpad[nxt][:, :, 0:2], xpad[cur][:, :, 128:130], [a], [b_])

    def zero_halo(slot, left):
        sl = slice(0, 2) if left else slice(130, 132)
        S.op("pool", lambda g: g.memset(xpad[slot][:, :, sl], 0.0), writes=[buf(XN[slot])])

    def conv_silu(slot, nchunks):
        xs_ = XN[slot]
        chunks = list(range(nchunks))
        for r0 in range(0, len(chunks), 4):
            grp = chunks[r0:r0 + 4]
            bn = nextbank()
            for ci, ch in enumerate(grp):
                for kk in range(5):
                    mm(P[bn][:, ci * 128:(ci + 1) * 128], dconv[:, ch * 5 + kk, :], xpad[slot][:, ch, kk:kk + 128],
                       (ci == 0 and kk == 0), ["dconv", xs_], bn)
            for ci, ch in enumerate(grp):
                act(xbcT[:, ch, :], P[bn][:, ci * 128:(ci + 1) * 128], AF.Silu, [bn, "cf"], ["xbcT"],
                    bias=cf[:, C_CB + ch:C_CB + ch + 1])

    def smc(ss, off, n=16):
        b0 = 32 + ss * 160 + off
        return sm[:, b0:b0 + n]

    def dt_and_decay(slot, dcol, bwd, ss=None):
        ss = slot if ss is None else ss
        hs = HN[slot]
        n_ = lambda t_: "sm_%s%d" % (t_, ss)
        sb_ = "smb%d" % ss
        ahi, alo = smb[:, ss * 32:ss * 32 + 16], smb[:, ss * 32 + 16:ss * 32 + 32]
        for kc in range(8):
            mm(P["PS"][:, 0:16], hT[slot][:, kc, :], w_sb[:, kc, TM_DT + dcol:TM_DT + dcol + 16], kc == 0,
               ["w_rev", hs], "PS")
        tt("dve", smc(ss, 128), P["PS"][:, 0:16], cf[:, C_DTB + dcol:C_DTB + dcol + 16], ALU.add,
           ["PS", "cf"], [n_("t")])
        act(smc(ss, 128), smc(ss, 128), AF.Exp, [n_("t")], [n_("t")])
        act(smc(ss, 0), smc(ss, 128), AF.Ln, [n_("t")], [n_("dt")], bias=1.0)
        yield
        tt("dve", smc(ss, 16), smc(ss, 0), A_bc[:, dcol:dcol + 16], ALU.mult, [n_("dt"), "A_bc"], [n_("a")])
        cp("dve", ahi, smc(ss, 16), [n_("a")], [sb_])
        tt("dve", smc(ss, 128), smc(ss, 16), ahi, ALU.subtract, [n_("a"), sb_], [n_("t")])
        cp("dve", alo, smc(ss, 128), [n_("t")], [sb_])
        yield
        tri = cb[:, B_TRB:B_TRB + 128] if bwd else cb[:, B_TRF:B_TRF + 128]
        mm(P["PS"][:, 0:16], tri, ahi, True, ["cb", sb_], "PS")
        mm(P["PS"][:, 0:16], tri, alo, False, ["cb", sb_], "PS")
        mm(P["PS"][:, 16:32], cb[:, B_ONE:B_ONE + 128], ahi, False, ["cb", sb_], "PS")
        mm(P["PS"][:, 16:32], cb[:, B_ONE:B_ONE + 128], alo, False, ["cb", sb_], "PS")
        cp("dve", smc(ss, 32), P["PS"][:, 0:16], ["PS"], [n_("cs")])
        ts("dve", smc(ss, 48), P["PS"][:, 0:16], -1.0, None, ALU.mult, ALU.bypass, ["PS"], [n_("ncs")])
        tt("dve", smc(ss, 128), P["PS"][:, 16:32], smc(ss, 32), ALU.subtract, ["PS", n_("cs")], [n_("t")])
        act(smc(ss, 64), P["PS"][:, 0:16], AF.Exp, ["PS"], [n_("ecs")])
        act(smc(ss, 96), P["PS"][:, 16:32], AF.Exp, ["PS"], [n_("etot")])
        act(smc(ss, 80), smc(ss, 128), AF.Exp, [n_("t")], [n_("dec")])
        yield
        tt("dve", smc(ss, 112), smc(ss, 0), smc(ss, 80), ALU.mult, [n_("dt"), n_("dec")], [n_("dtdec")])
        yield

    def ssd(ss, bwd, states_only, with_skip):
        n_ = lambda t_: "sm_%s%d" % (t_, ss)
        sb_ = "smb%d" % ss
        for c in range(8):
            tr(PT[:, c * 128:(c + 1) * 128], xbcT[:, c, :], ["xbcT"], "PT")
        ptv = PT[:].rearrange("p (h d) -> p h d", h=16)
        if not states_only:
            tt("dve", XX[:, 0, :].rearrange("p (h d) -> p h d", h=16), ptv, bc3(smc(ss, 0), 64), ALU.mult,
               ["PT", n_("dt")], ["XX"])
        tt("dve", XX[:, 1, :].rearrange("p (h d) -> p h d", h=16), ptv, bc3(smc(ss, 112), 64), ALU.mult,
           ["PT", n_("dtdec")], ["XX"])
        for g in range(2):
            tr(PT[:, g * 128:(g + 1) * 128], xbcT[:, 8 + g, :], ["xbcT"], "PT")
        cp("dve", Bt[:].rearrange("p a b -> p (a b)"), PT[:, 0:256], ["PT"], ["Bt"])
        yield
        if not states_only:
            for g in range(2):
                mm(P["PS"][:, 128 + g * 128:128 + (g + 1) * 128], xbcT[:, 8 + g, :], xbcT[:, 10 + g, :],
                   g == 0, ["xbcT"], "PS")
            mcol = C_M01B if bwd else C_M01F
            tt("dve", CBm[:], P["PS"][:, 128:384].rearrange("p (a b) -> p a b", a=2),
               cf[:, mcol:mcol + 128].unsqueeze(1).to_broadcast([128, 2, 128]), ALU.mult, ["PS", "cf"], ["CBm"])
            yield
            mn = B_MNB if bwd else B_MNF
            tri = cb[:, B_TRB:B_TRB + 128] if bwd else cb[:, B_TRF:B_TRF + 128]
            def grp(g, pyb, lsfix):
                if with_skip:
                    for c in range(4):
                        mm(P[pyb][:, c * 128:(c + 1) * 128], xbcT[:, g * 4 + c, :], dskd[:, g * 4 + c, :],
                           c == 0, ["xbcT", "dskd"], pyb)
                def hb_a(hb):
                    bn = nextbank()
                    mm(P[bn][:], cb[:, B_ID:B_ID + 128], cb[:, mn:mn + 512], True, ["cb"], bn)
                    for j in range(4):
                        h = hb * 4 + j
                        for part in range(2):
                            col = ss * 32 + part * 16 + h
                            mm(P[bn][:, j * 128:(j + 1) * 128], smb[:, col:col + 1].to_broadcast([128, 128]), tri,
                               False, ["cb", sb_], bn)
                    ls = (hb % 2) if lsfix is None else lsfix
                    ln_ = "Lt%d" % ls
                    for j in range(4):
                        h = hb * 4 + j
                        act(Lt[ls][:, j, :], P[bn][:, j * 128:(j + 1) * 128], AF.Exp, [bn, n_("ncs")], [ln_],
                            bias=smc(ss, 48 + h, 1))
                    tt("dve", Lt[ls][:], Lt[ls][:], CBm[:, g:g + 1, :].to_broadcast([128, 4, 128]), ALU.mult,
                       [ln_, "CBm"], [ln_])

                def hb_b(hb):
                    ls = (hb % 2) if lsfix is None else lsfix
                    ln_ = "Lt%d" % ls
                    for j in range(4):
                        h = hb * 4 + j
                        first = (not with_skip) and (h % 8 == 0)
                        mm(P[pyb][:, (h % 8) * 64:(h % 8 + 1) * 64], Lt[ls][:, j, :], XX[:, 0, h * 64:(h + 1) * 64],
                           first, [ln_, "XX"], pyb)

                if lsfix is None:
                    hb_a(2 * g)
                    yield
                    hb_a(2 * g + 1)
                    yield
                    hb_b(2 * g)
                    yield
                    hb_b(2 * g + 1)
                    yield
                else:
                    for hb in (2 * g, 2 * g + 1):
                        hb_a(hb)
                        yield
                        hb_b(hb)
                        yield
                bn = nextbank()
                mm(P[bn][:], xbcT[:, 10 + g, :], Sbf[:, g * 512:(g + 1) * 512], True, ["xbcT", "Sbf"], bn)
                tt("dve", y1[:, g * 512:(g + 1) * 512].rearrange("p (h d) -> p h d", h=8),
                   P[bn][:].rearrange("p (h d) -> p h d", h=8), bc3(smc(ss, 64 + g * 8, 8), 64),
                   ALU.mult, [bn, n_("ecs")], ["y1"])
                tt("dve", y1[:, g * 512:(g + 1) * 512], y1[:, g * 512:(g + 1) * 512], P[pyb][:], ALU.add,
                   ["y1", pyb], ["y1"])
                yield
            if bwd:
                gg_ = [grp(0, "PY", 0), grp(1, "OA", 1)]
                while gg_:
                    for q_ in list(gg_):
                        try:
                            next(q_)
                        except StopIteration:
                            gg_.remove(q_)
                    yield
            else:
                for g in range(2):
                    yield from grp(g, "PY", None)
        tt("pool", Sst[:].rearrange("p (h d) -> p h d", h=16), Sst[:].rearrange("p (h d) -> p h d", h=16),
           bc3(smc(ss, 96), 64), ALU.mult, ["Sst", n_("etot")], ["Sst"])
        for g in range(2):
            bn = nextbank()
            mm(P[bn][:], Bt[:, g, :], XX[:, 1, g * 512:(g + 1) * 512], True, ["Bt", "XX"], bn)
            tt("dve", Sst[:, g * 512:(g + 1) * 512], Sst[:, g * 512:(g + 1) * 512], P[bn][:], ALU.add,
               ["Sst", bn], ["Sst"])
        cp("dve", Sbf[:], Sst[:], ["Sst"], ["Sbf"])
        yield

    def attention(i, has_prev):
        blocks = ([0] if has_prev else []) + [1, 2]
        obanks = ["OA", "OB", "OC"]

        def oslot(h):
            return obanks[h // 7], (h % 7) * 65

        firstw = {b_: True for b_ in obanks}
        bx = ("PA", "PB")
        rounds = [(gp, o) for gp in range(2) for o in blocks]

        def step_a(r):
            gp, o = rounds[r]
            ks = (i + o - 1) % 3
            for par in range(2):
                bi0 = gp * 8 + par * 4
                mm(P[bx[par]][:], cb[:, B_ID:B_ID + 128],
                   biasT[:, o, bi0:bi0 + 4, :].rearrange("p a b -> p (a b)"), True, ["cb", "biasT"], bx[par])
            for k4 in range(4):
                for par in range(2):
                    h = 8 * gp + 2 * k4 + par
                    g, j = h // 4, h // 2
                    mm(P[bx[par]][:, k4 * 128:(k4 + 1) * 128], KT[ks][par * 64:(par + 1) * 64, g, :],
                       QT[par * 64:(par + 1) * 64, j, :], False, ["KT%d" % ks, "QT"], bx[par])
            for par in range(2):
                pi_ = (r % 2) * 2 + par
                act(pT[pi_][:], P[bx[par]][:], AF.Exp, [bx[par]], ["pT%d" % pi_])

        def step_b(r):
            gp, o = rounds[r]
            ks = (i + o - 1) % 3
            for par in range(2):
                pi_ = (r % 2) * 2 + par
                for k4 in range(4):
                    h = 8 * gp + 2 * k4 + par
                    ob, oc = oslot(h)
                    mm(P[ob][:, oc:oc + 65], pT[pi_][:, k4 * 128:(k4 + 1) * 128], Va[ks][:, h // 4, :],
                       firstw[ob], ["pT%d" % pi_, "Va%d" % ks], ob)
                    firstw[ob] = False

        step_a(0)
        yield
        for r in range(len(rounds)):
            if r + 1 < len(rounds):
                step_a(r + 1)
                yield
            step_b(r)
            yield

    def attn_epi():
        obanks = ["OA", "OB", "OC"]
        for bi, ob in enumerate(obanks):
            nh = 7 if bi < 2 else 2
            v = P[ob][:, 0:nh * 65].rearrange("p (h d) -> p h d", h=nh)
            tt("dve", sm[:, 352 + bi * 7:352 + bi * 7 + nh].unsqueeze(2), v[:, :, 64:65],
               esink[:, bi * 7:bi * 7 + nh].unsqueeze(2), ALU.add, [ob, "esink"], ["sm_den"])
        S.op("dve", lambda g_: g_.reciprocal(sm[:, 368:384], sm[:, 352:368]), reads=[buf("sm_den")],
             writes=[buf("sm_rden")])
        for bi, ob in enumerate(obanks):
            nh = 7 if bi < 2 else 2
            v = P[ob][:, 0:nh * 65].rearrange("p (h d) -> p h d", h=nh)
            tt("dve", o1[:, bi * 448:bi * 448 + nh * 64].rearrange("p (h d) -> p h d", h=nh), v[:, :, 0:64],
               bc3(sm[:, 368 + bi * 7:368 + bi * 7 + nh], 64), ALU.mult, [ob, "sm_rden"], ["o1"])
        yield
        tt("dve", o1[:], o1[:], gas[:], ALU.mult, ["o1", "gas"], ["o1"])
        sumsq(o1[:], D, 8, ["o1"], "sm_ssa")
        yield
        rstd_of(sm[:, 8:9], D, sm[:, 9:10], ["sm_ssa"], ["sm_rsa"])
        yield
        act(catbf[:, D:2 * D], o1[:], AF.Copy, ["o1", "sm_rsa"], ["catbf"], scale=sm[:, 9:10])
        yield

    def proj_zg(slot):
        hs = HN[slot]
        for (col0, dst, dname) in ((TM_Z, zs, "zs"), (TM_GA, gas, "gas")):
            for hf in range(2):
                bn = nextbank()
                for kc in range(8):
                    mm(P[bn][:], hT[slot][:, kc, :], w_sb[:, kc, col0 + hf * 512:col0 + (hf + 1) * 512], kc == 0,
                       ["w_fwd", hs], bn)
                act(dst[:, hf * 512:(hf + 1) * 512], P[bn][:], AF.Silu, [bn], [dname])
                yield

    def proj_q(slot):
        hs = HN[slot]
        for r in range(2):
            bn = nextbank()
            for c in range(4):
                ch = r * 4 + c
                for kc in range(8):
                    mm(P[bn][:, c * 128:(c + 1) * 128], w_sb[:, kc, FM + 1536 + ch * 128:FM + 1536 + (ch + 1) * 128],
                       hT[slot][:, kc, :], (c == 0 and kc == 0), ["w_fwd", hs], bn)
            cp("dve", QT[:, r * 4:(r + 1) * 4, :].rearrange("p a b -> p (a b)"), P[bn][:], [bn], ["QT"])
            yield

    def ssd_fwd_main(i):
        dma(t2[:], yb_d[i * T:(i + 1) * T, :], ["ybwd_dram"], ["t2"])
        yield from ssd(i % 2, False, False, True)

    def ssd_epi(i):
        tt("dve", y1[:], y1[:], t2[:], ALU.add, ["y1", "t2"], ["y1"])
        tt("dve", y1[:], y1[:], zs[:], ALU.mult, ["y1", "zs"], ["y1"])
        dma(t2[:], x_d[i * T:(i + 1) * T, :], [], ["t2"])
        yield
        for g in range(2):
            sumsq(y1[:, g * 512:(g + 1) * 512], 512, 4 + g, ["y1"], "sm_ssy%d" % g)
        rstd_of(sm[:, 4:6], 512, sm[:, 6:8], ["sm_ssy0", "sm_ssy1"], ["sm_rsy"])
        yield
        for g in range(2):
            act(catbf[:, g * 512:(g + 1) * 512], y1[:, g * 512:(g + 1) * 512], AF.Copy, ["y1", "sm_rsy"], ["catbf"],
                scale=sm[:, 6 + g:7 + g])
        yield

    def after(flags, keys, g):
        while not all(flags.get(k_) for k_ in keys):
            yield
        yield from g

    def par(*gs):
        gens = [g for g in gs if g is not None]
        while gens:
            for g in list(gens):
                try:
                    next(g)
                except StopIteration:
                    gens.remove(g)
            yield

    def epi_chain(i, extra=None, flags=None):
        yield from par(ssd_epi(i), attn_epi())
        for _ in range(int(os.environ.get("KWARM_TAIL", "12"))):
            mm(P["OC"][:, 256:512], cb[:, B_ID:B_ID + 128], cb[:, B_MNF:B_MNF + 256], False, ["cb"], "OC")
        yield from par(tail_H(i, flags), extra)

    def tail_H(i, flags=None):
        catT = XX[:].rearrange("p a (c t) -> p (a c) t", t=T)
        for r in range(2):
            for c in range(8):
                tr(PT[:, c * 128:(c + 1) * 128], catbf[:, (r * 8 + c) * 128:(r * 8 + c + 1) * 128], ["catbf"], "PT")
            cp("act" if r else "dve", catT[:, r * 8:(r + 1) * 8, :].rearrange("p a b -> p (a b)"), PT[:], ["PT"],
               ["XX"])
            yield
        for hf in range(2):
            bn = nextbank()
            for kc in range(16):
                mm(P[bn][:], catT[:, kc, :], wo_sb[:, kc, hf * 512:(hf + 1) * 512], kc == 0, ["XX", "wo_sb"], bn)
            tt("dve", t2[:, hf * 512:(hf + 1) * 512], t2[:, hf * 512:(hf + 1) * 512], P[bn][:], ALU.add,
               ["t2", bn], ["t2"])
            yield
        if flags is not None:
            flags["op"] = True
        sumsq(t2[:], D, 10, ["t2"], "sm_sso")
        rstd_of(sm[:, 10:11], D, sm[:, 11:12], ["sm_sso"], ["sm_rso"])
        S.op("dve", lambda g_: g_.scalar_tensor_tensor(out=y1[:], in0=t2[:], scalar=sm[:, 11:12],
                                                       in1=cf[:, C_FG:C_FG + D], op0=ALU.mult, op1=ALU.mult),
             reads=[buf("t2"), buf("sm_rso"), buf("cf")], writes=[buf("y1")])
        dma(out_d[i * T:(i + 1) * T, :], y1[:], ["y1"], ["out_dram"])
        yield

    def ssd_bwd_chain(i, states_only):
        yield from ssd(i % 2, True, states_only, False)
        if not states_only:
            dma(yb_d[i * T:(i + 1) * T, :], y1[:], ["y1"], ["ybwd_dram"])
        yield

    _run([front(NT - 1, (NT - 1) % 3, False)])
    zero_halo((NT - 1) % 3, left=False)
    _run([front(NT - 2, (NT - 2) % 3, False), dt_and_decay((NT - 1) % 3, 16, True, ss=(NT - 1) % 2)])
    halo((NT - 1) % 3, (NT - 2) % 3, True)
    _run([front(NT - 3, (NT - 3) % 3, False)])
    halo((NT - 2) % 3, (NT - 3) % 3, True)
    bg = bg_setup()
    for i in reversed(range(NT)):
        slot = i % 3
        so = (i >= NTH)
        conv_silu(slot, 10 if so else 12)
        if not so:
            dma(xbc_d[i], xbcT[:].rearrange("p a b -> p (a b)"), ["xbcT"], ["xbc_dram"])
        gens = []
        if i - 3 >= 0:
            gens.append(front(i - 3, slot, False))
        if i - 1 >= 0:
            gens.append(dt_and_decay((i - 1) % 3, 16, True, ss=(i - 1) % 2))
        if so:
            gens.append(ssd_bwd_chain(i, so))
        else:
            gens.insert(0, ssd_bwd_chain(i, so))
        rv = os.environ.get("KWARM_REV", "0")
        S.warm_fn = warm if (NWARM and (rv == "1" or (rv == "own" and not so) or (rv == "far" and so))) else None
        _run(gens, bg if so else None)
        S.warm_fn = None
        if i == NTH:
            _run([bg])
        if i - 3 >= 0:
            halo((i - 2) % 3, slot, True)
            if i - 3 == 0:
                zero_halo(slot, left=True)

    S.op("pool", lambda g: g.memset(Sst[:], 0.0), writes=[buf("Sst")])
    S.op("pool", lambda g: g.memset(Sbf[:], 0.0), writes=[buf("Sbf")])
    def load_xbc(i):
        dma(xbcT[:].rearrange("p a b -> p (a b)"), xbc_d[i], ["xbc_dram"], ["xbcT"])

    def front_kvonly(i, slot):
        yield from front(i, slot, True, head_only=True)
        yield from front_kv(i, slot)

    _run([front_kvonly(0, 0)])
    load_xbc(0)
    _run([front_kvonly(1, 1), dt_and_decay(0, 0, False), proj_q(0)])
    _run([proj_zg(0)])
    def one_step(g):
        next(g)
        yield

    g_ssd = ssd_fwd_main(0)
    for i in range(NTH):
        slot = i % 2
        gens = [g_ssd, attention(i, i > 0)]
        if i + 2 <= NTH:
            gens.append(front(i + 2, slot, True, head_only=True))
        if i + 1 < NTH:
            gens.append(dt_and_decay(1 - slot, 0, False))
        S.warm_fn = warm if NWARM else None
        _run(gens)
        S.warm_fn = None
        if i + 1 < NTH:
            load_xbc(i + 1)
        fl = {}
        gens = [epi_chain(i, proj_zg(1 - slot) if i + 1 < NTH else None, fl)]
        if i + 2 <= NTH:
            gens.append(front_kv(i + 2, slot))
        if i + 1 < NTH:
            gens.append(proj_q(1 - slot))
            g_ssd = ssd_fwd_main(i + 1)
            gens.append(after(fl, ["op"], one_step(g_ssd)))
        _run(gens)
    S.finish()
    return nc


def _t5_bucket_table():
    import jax
    import jax.numpy as jnp
    with jax.default_device(jax.devices("cpu")[0]):
        return _t5_bucket_table_impl(jnp)


def _t5_bucket_table_impl(jnp):
    rel = jnp.arange(-128, 129)
    half = 16
    ret = (rel > 0).astype(jnp.int32) * half
    n = jnp.abs(rel)
    is_small = n < 8
    nf = jnp.maximum(n, 1).astype(jnp.float32)
    large = 8 + (jnp.log(nf / 8) / math.log(128 / 8) * (half - 8)).astype(jnp.int32)
    large = jnp.minimum(large, half - 1)
    return np.asarray(ret + jnp.where(is_small, n, large))


def _consts_bf():
    idx = np.arange(128)
    c = np.zeros((128, NCB), np.float32)
    c[:, B_ID:B_ID + 128] = np.eye(128)
    c[:, B_TRF:B_TRF + 128] = (idx[:, None] <= idx[None, :])
    c[:, B_TRB:B_TRB + 128] = (idx[:, None] >= idx[None, :])
    c[:, B_ONE:B_ONE + 128] = 1.0
    c[:, B_MNF:B_MNF + 512] = np.tile(np.where(idx[:, None] > idx[None, :], NEG, 0.0), (1, 4))
    c[:, B_MNB:B_MNB + 512] = np.tile(np.where(idx[:, None] < idx[None, :], NEG, 0.0), (1, 4))
    return c.astype(ml_dtypes.bfloat16)


def _prep_core(inp, b, half, NTH, bucket):
    f32 = np.float32
    x = inp["x"][b]
    if half:
        x = x[::-1]
    x = np.ascontiguousarray(x, dtype=f32)
    w_in = np.asarray(inp["w_in"], f32)
    z, xbc, dtc = w_in[:, 0:1024], w_in[:, 1024:2560], w_in[:, 2560:2592]
    q, k, v, ga = w_in[:, 2592:3616], w_in[:, 3616:3872], w_in[:, 3872:4128], w_in[:, 4128:5152]
    d0, d1 = dtc[:, 0:16], dtc[:, 16:32]
    if half:
        d0, d1 = d1, d0
    kd = np.concatenate([np.concatenate([k[:, i * 64:(i + 1) * 64]] * 2, axis=1) for i in range(4)], axis=1)
    w = np.ascontiguousarray(np.concatenate([z, ga, v, d0, d1, xbc, q, kd], axis=1))
    assert w.shape[1] == NCOLS
    idx = np.arange(128)
    cfa = np.zeros((128, NCF), f32)
    cfa[:, C_M01F:C_M01F + 128] = (idx[:, None] <= idx[None, :])
    cfa[:, C_M01B:C_M01B + 128] = (idx[:, None] >= idx[None, :])
    order = [1, 0] if half else [0, 1]
    cfa[:, C_DTB:C_DTB + 32] = np.asarray(inp["dt_bias"], f32)[order].reshape(1, 32)
    cfa[:, C_ALOG:C_ALOG + 32] = np.asarray(inp["a_log"], f32)[order].reshape(1, 32)
    cfa[:, C_SINK:C_SINK + 16] = np.asarray(inp["sink"], f32).reshape(1, 16)
    cfa[:, C_FG:C_FG + D] = np.asarray(inp["final_norm_g"], f32).reshape(1, D)
    cfa[:, C_GIN:C_GIN + 8] = np.asarray(inp["norm_in_g"], f32).reshape(8, 128).T
    cfa[:, C_GSSD:C_GSSD + 8] = np.asarray(inp["ssd_norm_g"], f32).reshape(8, 128).T
    cfa[:, C_GATT:C_GATT + 8] = np.asarray(inp["attn_norm_g"], f32).reshape(8, 128).T
    cfa[:, C_CB:C_CB + 12] = np.asarray(inp["conv_b"], f32).reshape(12, 128).T
    cw = np.asarray(inp["conv_w"], f32)
    if half:
        cw = cw[::-1]
    cfa[:, C_CW:C_CW + 60] = cw.reshape(5, 12, 128).transpose(2, 1, 0).reshape(128, 60)
    cfa[:, C_DSK:C_DSK + 8] = np.repeat(np.asarray(inp["d_skip"], f32), 64).reshape(8, 128).T
    cfa[:, C_NH] = -0.5
    rb = np.asarray(inp["rel_bias"], f32)
    tq = idx[:, None] - idx[None, :]
    bt = np.full((128, 3, 16, 128), NEG, f32)
    for o in range(3):
        rl = (o - 1) * 128 + tq
        valid = np.abs(rl) <= 128
        rg = -rl if half else rl
        bidx = bucket[np.clip(rg, -128, 128) + 128]
        vals = rb[bidx]
        hperm = [8 * gp + 2 * k4 + par for gp in range(2) for par in range(2) for k4 in range(4)]
        bt[:, o] = np.where(valid[:, None, :], vals.transpose(0, 2, 1)[:, hperm, :], NEG)
    return {
        "x": x, "w_in": w, "w_out": np.ascontiguousarray(np.asarray(inp["w_out"], f32)[0]),
        "cst_f32": cfa, "cst_bf": _consts_bf(),
        "biasT": np.ascontiguousarray(bt.reshape(128, -1)).astype(ml_dtypes.bfloat16),
    }


_NC_CACHE = {}


def kernel(**inputs):
    NTH = NTH_FULL
    x = np.asarray(inputs["x"])
    Bsz, Sq, _ = x.shape
    assert Sq == 2 * NTH * T and Bsz == 4
    bucket = _t5_bucket_table()
    in_maps = [_prep_core(inputs, c // 2, c % 2, NTH, bucket) for c in range(8)]
    if NTH not in _NC_CACHE:
        _NC_CACHE[NTH] = _build(NTH)
    nc = _NC_CACHE[NTH]
    res = run_bass_kernel_spmd(nc, in_maps, core_ids=list(range(8)))
    out = np.empty((Bsz, Sq, D), np.float32)
    h = NTH * T
    for c in range(8):
        o = np.asarray(res.results[c]["out"], np.float32)
        if c % 2 == 0:
            out[c // 2, 0:h] = o
        else:
            out[c // 2, h:] = o[::-1]
    return out
```
